# Optimizing a Trainium2 kernel written in Bass

```python
import math
import jax, jax.numpy as jnp
from jax import lax
import numpy as np

D_MODEL = 1024
BATCH = 8
SEQ = 2048
DEPTH = 2
DEC_BATCH = 128
DEC_SEQ = 4
PAST_LEN = 16384
PAGE_SIZE = 128

MIX_WIDTH = D_MODEL // 2
N_BRANCH = 3
POOL_WINDOWS = (2, 4, 8, 16)
POOL_GROUPS = len(POOL_WINDOWS)
POOL_BUF = max(POOL_WINDOWS) - 1
DN_HEADS = 4
DN_HEAD_DIM = MIX_WIDTH // DN_HEADS
DN_CONV = 4
DN_CHUNK = 64
CM_WIDTH = MIX_WIDTH
CM_KERNEL = 31
D_FF = ((8 * D_MODEL // 3 + 127) // 128) * 128
PLE_DIM = 256
RMS_EPS = 1e-6
LN_EPS = 1e-5
IN_SPLITS = (MIX_WIDTH, 4 * MIX_WIDTH, 5 * MIX_WIDTH, 5 * MIX_WIDTH + DN_HEADS,
             5 * MIX_WIDTH + 2 * DN_HEADS, 5 * MIX_WIDTH + 2 * DN_HEADS + 2 * CM_WIDTH)
IN_WIDTH = IN_SPLITS[-1] + N_BRANCH * D_MODEL

kernel_name = 'hybrid_pool_gdn_conformer_step'


def rmsnorm(x, g, eps=RMS_EPS):
    xf = x.astype(jnp.float32)
    y = xf * lax.rsqrt(jnp.mean(xf * xf, axis=-1, keepdims=True) + eps) * g.astype(jnp.float32)
    return y.astype(x.dtype)


def swiglu(x, w_up, w_down):
    a, b = jnp.split(x @ w_up, 2, axis=-1)
    return (jax.nn.silu(a) * b) @ w_down


def l2norm(x, eps=1e-6):
    return x * lax.rsqrt(jnp.sum(x * x, axis=-1, keepdims=True) + eps)


def causal_dwconv(prefix, x, w, b=None):
    width, ch = w.shape
    ext = jnp.concatenate([prefix.astype(x.dtype), x], axis=1)
    y = lax.conv_general_dilated(ext, w.astype(x.dtype)[:, None, :], (1,), 'VALID',
                                 dimension_numbers=('NWC', 'WIO', 'NWC'), feature_group_count=ch)
    if b is not None:
        y = y + b.astype(x.dtype)
    return y, ext[:, ext.shape[1] - (width - 1):]


def multiscale_pool(prefix, x, pos0, w_grp, scale):
    B, L, C = x.shape
    P = prefix.shape[1]
    gw = C // POOL_GROUPS
    ext = jnp.concatenate([prefix.astype(x.dtype), x], axis=1)
    cs = jnp.cumsum(ext.astype(jnp.float32), axis=1)
    cs0 = jnp.concatenate([jnp.zeros((B, 1, C), jnp.float32), cs], axis=1)
    pos = pos0 + jnp.arange(L)
    means = []
    for gi, w in enumerate(POOL_WINDOWS):
        sl = slice(gi * gw, (gi + 1) * gw)
        s = cs0[:, P + 1:P + 1 + L, sl] - cs0[:, P + 1 - w:P + 1 - w + L, sl]
        cnt = jnp.minimum(pos + 1, w).astype(jnp.float32)[None, :, None]
        means.append(s / cnt)
    mean = jnp.concatenate(means, axis=-1)
    d = (mean - x.astype(jnp.float32)).astype(x.dtype).reshape(B, L, POOL_GROUPS, gw)
    y = jnp.einsum('blgc,gce->blge', d, w_grp).reshape(B, L, C) * scale
    return y, ext[:, ext.shape[1] - P:]


def gated_delta_chunked(q, k, v, g, beta, S0):
    B, L, H, DK = q.shape
    DV = v.shape[-1]
    C = min(DN_CHUNK, L)
    n = -(-L // C)
    pad = n * C - L

    def prep(t):
        t = jnp.pad(t, [(0, 0), (0, pad)] + [(0, 0)] * (t.ndim - 2))
        t = jnp.moveaxis(t, 2, 1)
        return t.reshape(t.shape[:2] + (n, C) + t.shape[3:])

    q, k, v, g, beta = prep(q), prep(k), prep(v), prep(g), prep(beta)
    gc = jnp.cumsum(g, axis=-1)
    idx = jnp.arange(C)
    incl = idx[:, None] >= idx[None, :]
    strict = idx[:, None] > idx[None, :]
    decay = jnp.exp(jnp.where(incl, gc[..., :, None] - gc[..., None, :], -jnp.inf))
    kb = k * beta[..., None]
    lower = jnp.where(strict, jnp.einsum('bhnik,bhnjk->bhnij', kb, k) * decay, 0.0)
    eye = jnp.eye(C, dtype=jnp.float32)
    T = lax.linalg.triangular_solve(eye + lower, jnp.broadcast_to(eye, lower.shape),
                                    left_side=True, lower=True, unit_diagonal=True)
    u = jnp.einsum('bhnij,bhnjv->bhniv', T, v * beta[..., None])
    w = jnp.einsum('bhnij,bhnjk->bhnik', T, kb * jnp.exp(gc)[..., None])
    attn = jnp.einsum('bhnik,bhnjk->bhnij', q, k) * decay
    g_last = gc[..., -1]
    k_tail = k * jnp.exp(g_last[..., None] - gc)[..., None]
    q_dec = q * jnp.exp(gc)[..., None]

    def step(S, xs):
        qd, wc, uc, ac, kt, gl = xs
        v_new = uc - jnp.einsum('bhck,bhkv->bhcv', wc, S)
        o = jnp.einsum('bhck,bhkv->bhcv', qd, S) + jnp.einsum('bhij,bhjv->bhiv', ac, v_new)
        S = S * jnp.exp(gl)[..., None, None] + jnp.einsum('bhck,bhcv->bhkv', kt, v_new)
        return S, o

    xs = tuple(jnp.moveaxis(t, 2, 0) for t in (q_dec, w, u, attn, k_tail, g_last))
    S, o = lax.scan(step, S0, xs)
    o = jnp.moveaxis(o, 0, 2).reshape(B, H, n * C, DV)[:, :, :L]
    return jnp.moveaxis(o, 1, 2), S


def gated_deltanet(prefix_conv, S0, qkv, z, a, b, conv_w, A_log, dt_bias, norm_g):
    B, L, _ = qkv.shape
    qkv_c, new_conv = causal_dwconv(prefix_conv, qkv, conv_w)
    qkv_c = jax.nn.silu(qkv_c.astype(jnp.float32))
    q, k, v = jnp.split(qkv_c, 3, axis=-1)
    q = l2norm(q.reshape(B, L, DN_HEADS, DN_HEAD_DIM)) * (DN_HEAD_DIM ** -0.5)
    k = l2norm(k.reshape(B, L, DN_HEADS, DN_HEAD_DIM))
    v = v.reshape(B, L, DN_HEADS, DN_HEAD_DIM)
    g = -jnp.exp(A_log.astype(jnp.float32)) * jax.nn.softplus(a.astype(jnp.float32) + dt_bias.astype(jnp.float32))
    beta = jax.nn.sigmoid(b.astype(jnp.float32))
    o, S = gated_delta_chunked(q, k, v, g, beta, S0.astype(jnp.float32))
    o = o * lax.rsqrt(jnp.mean(o * o, axis=-1, keepdims=True) + RMS_EPS) * norm_g.astype(jnp.float32)
    o = o.reshape(B, L, DN_HEADS * DN_HEAD_DIM) * jax.nn.silu(z.astype(jnp.float32))
    return o.astype(qkv.dtype), new_conv, S


def conformer_conv(prefix, glu_in, dw_w, dw_b, ln_g, ln_b):
    a, gate = jnp.split(glu_in, 2, axis=-1)
    h = a * jax.nn.sigmoid(gate)
    y, new_buf = causal_dwconv(prefix, h, dw_w, dw_b)
    yf = y.astype(jnp.float32)
    mu = jnp.mean(yf, axis=-1, keepdims=True)
    var = jnp.mean(jnp.square(yf - mu), axis=-1, keepdims=True)
    yn = (yf - mu) * lax.rsqrt(var + LN_EPS) * ln_g.astype(jnp.float32) + ln_b.astype(jnp.float32)
    return jax.nn.silu(yn).astype(glu_in.dtype), new_buf


def decoder_layer(x, p, pos0, st_pool, st_dnconv, st_dn, st_cm, lw):
    B, L, D = x.shape
    x = x + 0.5 * swiglu(rmsnorm(x, lw['g_ffn1']), lw['w_ffn1_up'], lw['w_ffn1_down'])
    u = rmsnorm(x, lw['g_mix'])
    proj = u @ lw['w_in']
    x_pool, qkv, z, a, b, glu_in, gates = jnp.split(proj, IN_SPLITS, axis=-1)
    o_pool, new_pool = multiscale_pool(st_pool, x_pool, pos0, lw['pool_w'], lw['pool_scale'])
    o_dn, new_dnconv, new_dn = gated_deltanet(st_dnconv, st_dn, qkv, z, a, b, lw['dn_conv_w'],
                                              lw['dn_A_log'], lw['dn_dt_bias'], lw['dn_norm_g'])
    o_cm, new_cm = conformer_conv(st_cm, glu_in, lw['cm_dw_w'], lw['cm_dw_b'], lw['cm_ln_g'], lw['cm_ln_b'])
    br = jnp.stack([o_pool, o_dn, o_cm], axis=2)
    br = jnp.einsum('blnc,ncd->blnd', br, lw['w_branch'])
    gt = jax.nn.sigmoid(gates.astype(jnp.float32)).reshape(B, L, N_BRANCH, D)
    merged = jnp.sum(gt * br.astype(jnp.float32), axis=2).astype(x.dtype)
    x = x + merged @ lw['w_out']
    x = x + 0.5 * swiglu(rmsnorm(x, lw['g_ffn2']), lw['w_ffn2_up'], lw['w_ffn2_down'])
    pe = p.astype(x.dtype) @ lw['w_ple_proj']
    x = x + jax.nn.sigmoid(rmsnorm(x, lw['g_ple']) @ lw['w_ple_gate']) * pe
    return x, new_pool, new_dnconv, new_dn, new_cm


def setup_inputs(seed: int = 0) -> dict:
    key = jax.random.key(seed)
    ks = iter(jax.random.split(key, 40))
    f32 = jnp.float32

    def nrm(shape, scale):
        return jax.random.normal(next(ks), shape, f32) * scale

    def gain(shape):
        return 1.0 + nrm(shape, 0.05)

    L = DEPTH
    x_prompt = nrm((BATCH, SEQ, D_MODEL), 1.0)
    x_sample = nrm((DEC_BATCH, DEC_SEQ, D_MODEL), 1.0)
    state_pool = nrm((L, DEC_BATCH, POOL_BUF, MIX_WIDTH), 1.0)
    state_dn_conv = nrm((L, DEC_BATCH, DN_CONV - 1, 3 * MIX_WIDTH), 1.0)
    state_dn = nrm((L, DEC_BATCH, DN_HEADS, DN_HEAD_DIM, DN_HEAD_DIM), 0.1)
    state_cm_conv = nrm((L, DEC_BATCH, CM_KERNEL - 1, CM_WIDTH), 0.5)
    p_prompt = nrm((L, BATCH, SEQ, PLE_DIM), 1.0)
    p_sample = nrm((L, DEC_BATCH, DEC_SEQ, PLE_DIM), 1.0)
    g_ffn1 = gain((L, D_MODEL))
    w_ffn1_up = nrm((L, D_MODEL, 2 * D_FF), D_MODEL ** -0.5)
    w_ffn1_down = nrm((L, D_FF, D_MODEL), D_FF ** -0.5)
    g_mix = gain((L, D_MODEL))
    w_in = nrm((L, D_MODEL, IN_WIDTH), D_MODEL ** -0.5)
    pool_w = nrm((L, POOL_GROUPS, MIX_WIDTH // POOL_GROUPS, MIX_WIDTH // POOL_GROUPS), (MIX_WIDTH // POOL_GROUPS) ** -0.5)
    pool_scale = gain((L, MIX_WIDTH))
    dn_conv_w = nrm((L, DN_CONV, 3 * MIX_WIDTH), DN_CONV ** -0.5)
    dn_A_log = jnp.log(jax.random.uniform(next(ks), (L, DN_HEADS), f32, 1.0, 16.0))
    dt = jnp.exp(jax.random.uniform(next(ks), (L, DN_HEADS), f32, math.log(1e-3), math.log(1e-1)))
    dn_dt_bias = dt + jnp.log(-jnp.expm1(-dt))
    dn_norm_g = gain((L, DN_HEAD_DIM))
    cm_dw_w = nrm((L, CM_KERNEL, CM_WIDTH), CM_KERNEL ** -0.5)
    cm_dw_b = nrm((L, CM_WIDTH), 0.01)
    cm_ln_g = gain((L, CM_WIDTH))
    cm_ln_b = nrm((L, CM_WIDTH), 0.01)
    w_branch = nrm((L, N_BRANCH, MIX_WIDTH, D_MODEL), MIX_WIDTH ** -0.5)
    w_out = nrm((L, D_MODEL, D_MODEL), D_MODEL ** -0.5)
    g_ffn2 = gain((L, D_MODEL))
    w_ffn2_up = nrm((L, D_MODEL, 2 * D_FF), D_MODEL ** -0.5)
    w_ffn2_down = nrm((L, D_FF, D_MODEL), D_FF ** -0.5)
    g_ple = gain((L, D_MODEL))
    w_ple_gate = nrm((L, D_MODEL, D_MODEL), D_MODEL ** -0.5)
    w_ple_proj = nrm((L, PLE_DIM, D_MODEL), PLE_DIM ** -0.5)
    g_final = gain((D_MODEL,))
    return {'x_prompt': x_prompt, 'x_sample': x_sample, 'state_pool': state_pool,
            'state_dn_conv': state_dn_conv, 'state_dn': state_dn, 'state_cm_conv': state_cm_conv,
            'p_prompt': p_prompt, 'p_sample': p_sample,
            'g_ffn1': g_ffn1, 'w_ffn1_up': w_ffn1_up, 'w_ffn1_down': w_ffn1_down,
            'g_mix': g_mix, 'w_in': w_in, 'pool_w': pool_w, 'pool_scale': pool_scale,
            'dn_conv_w': dn_conv_w, 'dn_A_log': dn_A_log, 'dn_dt_bias': dn_dt_bias, 'dn_norm_g': dn_norm_g,
            'cm_dw_w': cm_dw_w, 'cm_dw_b': cm_dw_b, 'cm_ln_g': cm_ln_g, 'cm_ln_b': cm_ln_b,
            'w_branch': w_branch, 'w_out': w_out,
            'g_ffn2': g_ffn2, 'w_ffn2_up': w_ffn2_up, 'w_ffn2_down': w_ffn2_down,
            'g_ple': g_ple, 'w_ple_gate': w_ple_gate, 'w_ple_proj': w_ple_proj, 'g_final': g_final}


def reference(x_prompt, x_sample, state_pool, state_dn_conv, state_dn, state_cm_conv, p_prompt, p_sample,
              g_ffn1, w_ffn1_up, w_ffn1_down, g_mix, w_in, pool_w, pool_scale,
              dn_conv_w, dn_A_log, dn_dt_bias, dn_norm_g, cm_dw_w, cm_dw_b, cm_ln_g, cm_ln_b,
              w_branch, w_out, g_ffn2, w_ffn2_up, w_ffn2_down, g_ple, w_ple_gate, w_ple_proj, g_final):
    bp = x_prompt.shape[0]
    dt = x_prompt.dtype
    pool0 = jnp.zeros((bp, POOL_BUF, MIX_WIDTH), dt)
    dnconv0 = jnp.zeros((bp, DN_CONV - 1, 3 * MIX_WIDTH), dt)
    dn0 = jnp.zeros((bp, DN_HEADS, DN_HEAD_DIM, DN_HEAD_DIM), jnp.float32)
    cm0 = jnp.zeros((bp, CM_KERNEL - 1, CM_WIDTH), dt)
    hp, hs = x_prompt, x_sample
    pool_p, dnconv_p, dn_p, cm_p = [], [], [], []
    pool_s, dnconv_s, dn_s, cm_s = [], [], [], []
    for i in range(DEPTH):
        lw = {'g_ffn1': g_ffn1[i], 'w_ffn1_up': w_ffn1_up[i], 'w_ffn1_down': w_ffn1_down[i],
              'g_mix': g_mix[i], 'w_in': w_in[i], 'pool_w': pool_w[i], 'pool_scale': pool_scale[i],
              'dn_conv_w': dn_conv_w[i], 'dn_A_log': dn_A_log[i], 'dn_dt_bias': dn_dt_bias[i],
              'dn_norm_g': dn_norm_g[i], 'cm_dw_w': cm_dw_w[i], 'cm_dw_b': cm_dw_b[i],
              'cm_ln_g': cm_ln_g[i], 'cm_ln_b': cm_ln_b[i], 'w_branch': w_branch[i], 'w_out': w_out[i],
              'g_ffn2': g_ffn2[i], 'w_ffn2_up': w_ffn2_up[i], 'w_ffn2_down': w_ffn2_down[i],
              'g_ple': g_ple[i], 'w_ple_gate': w_ple_gate[i], 'w_ple_proj': w_ple_proj[i]}
        hp, a1, a2, a3, a4 = decoder_layer(hp, p_prompt[i], 0, pool0, dnconv0, dn0, cm0, lw)
        hs, b1, b2, b3, b4 = decoder_layer(hs, p_sample[i], PAST_LEN, state_pool[i], state_dn_conv[i],
                                           state_dn[i], state_cm_conv[i], lw)
        pool_p.append(a1); dnconv_p.append(a2); dn_p.append(a3); cm_p.append(a4)
        pool_s.append(b1); dnconv_s.append(b2); dn_s.append(b3); cm_s.append(b4)
    y_prompt = rmsnorm(hp, g_final)
    y_sample = rmsnorm(hs, g_final)
    return (y_prompt, y_sample,
            jnp.stack(pool_p), jnp.stack(dnconv_p), jnp.stack(dn_p), jnp.stack(cm_p),
            jnp.stack(pool_s), jnp.stack(dnconv_s), jnp.stack(dn_s), jnp.stack(cm_s))
```

```python
import numpy as np
import concourse.bass as bass
import concourse.mybir as mybir
from concourse.bass_utils import run_bass_kernel_spmd

F32 = mybir.dt.float32
BF16 = mybir.dt.bfloat16
ALU = mybir.AluOpType
AF = mybir.ActivationFunctionType

D = 1024
KC = 8
DFF = 2816
NPR = 2048
NSM = 64
NT = NPR + NSM
DEPTH = 2
NCORE = 8
WIN = 6664
RMS_EPS = 1e-6
LN_EPS = 1e-5
NEG = -1.0e5

C_ID = 0
C_ONES = 128
C_M1 = 256
C_NEGI = 384
C_STRICT = 512
C_M1S = 640
C_ALLS = 768
C_NEGIS = 896
C_STRICTS = 1024
C_SEQM = 1152
C_SEL = 1168
C_DSELB = C_SEL + 1024
C_DSELA = C_DSELB + 8
C_INVCNT = C_DSELA + 8
C_INVW = C_INVCNT + 64
NCON = C_INVW + 4

V_GFFN1 = 0
V_GMIX = 8
V_GFFN2 = 16
V_GPLE = 24
V_PSCALE = 32
V_CMB = 36
V_LNG = 40
V_LNB = 44
V_NORMG = 48
V_DNW = 49
V_CMW = 97
V_ALOG = 221
V_DTB = 222
V_GFIN = 223
NVEC = 231


def make_consts():
    c = np.zeros((128, NCON), np.float32)
    i = np.arange(128)
    c[:, C_ID:C_ID + 128] = np.eye(128)
    c[:, C_ONES:C_ONES + 128] = 1.0
    c[:, C_M1:C_M1 + 128] = (i[:, None] <= i[None, :])
    c[:, C_NEGI:C_NEGI + 128] = np.where(i[None, :] >= i[:, None], 0.0, NEG)
    c[:, C_STRICT:C_STRICT + 128] = (i[None, :] > i[:, None])
    j = np.arange(64)
    same = (j[:, None] % 16) == (j[None, :] % 16)
    m1s = np.zeros((128, 128), np.float32)
    m1s[:64, :64] = same & (j[:, None] <= j[None, :])
    c[:, C_M1S:C_M1S + 128] = m1s
    alls = np.zeros((128, 128), np.float32)
    alls[:64, :64] = same
    c[:, C_ALLS:C_ALLS + 128] = alls
    negs = np.full((128, 128), NEG, np.float32)
    negs[:64, :64] = np.where(same & (j[None, :] >= j[:, None]), 0.0, NEG)
    c[:, C_NEGIS:C_NEGIS + 128] = negs
    st = np.zeros((128, 128), np.float32)
    st[:64, :64] = same & (j[None, :] > j[:, None])
    c[:, C_STRICTS:C_STRICTS + 128] = st
    sq = np.zeros((128, 16), np.float32)
    sq[:64] = (j[:, None] % 16) == np.arange(16)[None, :]
    c[:, C_SEQM:C_SEQM + 16] = sq
    for r in range(8):
        c[r, C_SEL + r * 128:C_SEL + (r + 1) * 128] = 1.0
    for r in range(4, 8):
        c[r, C_DSELB + r] = 1.0
    for r in range(4):
        c[r, C_DSELA + r] = 1.0
    for g, w in enumerate((2, 4, 8, 16)):
        for t in range(16):
            c[:, C_INVCNT + g * 16 + t] = 1.0 / min(t + 1, w)
        c[:, C_INVW + g] = 1.0 / w
    return c


def make_vecs(inp):
    v = np.zeros((DEPTH, 128, NVEC), np.float32)
    for l in range(DEPTH):
        def pk(a, n):
            return np.ascontiguousarray(a.reshape(n, 128).T)
        v[l, :, V_GFFN1:V_GFFN1 + 8] = pk(inp['g_ffn1'][l], 8)
        v[l, :, V_GMIX:V_GMIX + 8] = pk(inp['g_mix'][l], 8)
        v[l, :, V_GFFN2:V_GFFN2 + 8] = pk(inp['g_ffn2'][l], 8)
        v[l, :, V_GPLE:V_GPLE + 8] = pk(inp['g_ple'][l], 8)
        v[l, :, V_PSCALE:V_PSCALE + 4] = pk(inp['pool_scale'][l], 4)
        v[l, :, V_CMB:V_CMB + 4] = pk(inp['cm_dw_b'][l], 4)
        v[l, :, V_LNG:V_LNG + 4] = pk(inp['cm_ln_g'][l], 4)
        v[l, :, V_LNB:V_LNB + 4] = pk(inp['cm_ln_b'][l], 4)
        v[l, :, V_NORMG] = inp['dn_norm_g'][l]
        v[l, :, V_DNW:V_DNW + 48] = inp['dn_conv_w'][l].reshape(4, 12, 128).transpose(2, 1, 0).reshape(128, 48)
        v[l, :, V_CMW:V_CMW + 124] = inp['cm_dw_w'][l].reshape(31, 4, 128).transpose(2, 1, 0).reshape(128, 124)
        v[l, 0:4, V_ALOG] = inp['dn_A_log'][l]
        v[l, 0:4, V_DTB] = inp['dn_dt_bias'][l]
        v[l, :, V_GFIN:V_GFIN + 8] = pk(inp['g_final'], 8)
    return v


class View:
    __slots__ = ('ap', 'space', 'runs')

    def __init__(self, ap, space, runs):
        self.ap = ap
        self.space = space
        self.runs = runs


class Rec:
    __slots__ = ('lo', 'hi', 'w', 'r')

    def __init__(self, lo, hi, w, r):
        self.lo = lo
        self.hi = hi
        self.w = w
        self.r = r


class Space:
    def __init__(self):
        self.recs = []

    def carve(self, lo, hi):
        out = []
        new = []
        cur = lo
        for r in self.recs:
            if r.hi <= lo or r.lo >= hi:
                new.append(r)
                continue
            if r.lo < lo:
                new.append(Rec(r.lo, lo, r.w, dict(r.r)))
                r.lo = lo
            right = None
            if r.hi > hi:
                right = Rec(hi, r.hi, r.w, dict(r.r))
                r.hi = hi
            if r.lo > cur:
                g = Rec(cur, r.lo, None, {})
                new.append(g)
                out.append(g)
            new.append(r)
            out.append(r)
            cur = r.hi
            if right is not None:
                new.append(right)
        if cur < hi:
            g = Rec(cur, hi, None, {})
            new.append(g)
            out.append(g)
        new.sort(key=lambda x: x.lo)
        self.recs = new
        return out

    def write_commit(self, lo, hi, tok):
        self.recs = [r for r in self.recs if r.hi <= lo or r.lo >= hi]
        self.recs.append(Rec(lo, hi, tok, {}))
        self.recs.sort(key=lambda x: x.lo)


class Prog:
    ENG = ('pe', 'act', 'dve', 'pool', 'sp')

    def __init__(self, nc, n_dma_sems=24):
        self.nc = nc
        self.semobj = {}
        self.own = {}
        for e in self.ENG:
            nm = 'sem_' + e
            self.semobj[nm] = nc.alloc_semaphore(nm)
            self.own[e] = nm
        self.cnt = {e: 0 for e in self.ENG}
        self.stream = {e: [] for e in self.ENG}
        self.waited = {e: {} for e in self.ENG}
        self.spaces = {'sb': Space(), 'ps': Space()}
        self.dma_pool = []
        self.dma_cnt = {}
        for i in range(n_dma_sems):
            nm = 'sem_dma%d' % i
            self.semobj[nm] = nc.alloc_semaphore(nm)
            self.dma_pool.append(nm)
            self.dma_cnt[nm] = 0
        self.dma_rr = 0
        self.dma_pool_sw = []
        for i in range(6):
            nm = 'sem_swdma%d' % i
            self.semobj[nm] = nc.alloc_semaphore(nm)
            self.dma_pool_sw.append(nm)
            self.dma_cnt[nm] = 0
        self.dma_rr_sw = 0
        self.out_tokens = []
        self.n_ops = 0
        self.n_waits = 0
        self.dry = False
        self.label = ''
        self.names = None

    def new_dma_sem(self, name):
        self.semobj[name] = self.nc.alloc_semaphore(name)
        self.dma_cnt[name] = 0
        return name

    def op(self, e, fn, reads=(), writes=(), count=True, dma=None, is_out=False):
        if self.dry:
            return None
        deps = {}
        own = self.own[e]

        def add(tok, raw):
            if tok is None:
                return
            s, v = tok
            if s == own and (e == 'pe' or not raw):
                return
            if deps.get(s, 0) < v:
                deps[s] = v

        rrecs = []
        ps_reads = [vw for vw in reads if vw is not None and vw.space == 'ps']
        reads = [vw for vw in reads if vw is not None and vw.space != 'ps']
        writes = list(writes) + ps_reads
        writes = [vw if vw.space != 'ps' else View(vw.ap, 'ps', [(lo // 2048 * 2048, (hi - 1) // 2048 * 2048 + 2048) for lo, hi in vw.runs]) for vw in writes]
        for vw in reads:
            sp = self.spaces[vw.space]
            for lo, hi in vw.runs:
                for r in sp.carve(lo, hi):
                    add(r.w, True)
                    rrecs.append(r)
        n_real_w = len(writes) - len(ps_reads)
        for wi, vw in enumerate(writes):
            sp = self.spaces[vw.space]
            for lo, hi in vw.runs:
                for r in sp.carve(lo, hi):
                    add(r.w, wi >= n_real_w)
                    for s, v in r.r.items():
                        add((s, v), False)
        dsem = None
        if dma is not None:
            if dma == 'auto' and e == 'pool':
                dsem = self.dma_pool_sw[self.dma_rr_sw % len(self.dma_pool_sw)]
                self.dma_rr_sw += 1
            elif dma == 'auto':
                dsem = self.dma_pool[self.dma_rr % len(self.dma_pool)]
                self.dma_rr += 1
            else:
                dsem = dma
            if self.dma_cnt[dsem] > 0:
                add((dsem, self.dma_cnt[dsem]), False)
            self.dma_cnt[dsem] += 16
            tok = (dsem, self.dma_cnt[dsem])
        elif count:
            self.cnt[e] += 1
            tok = (own, self.cnt[e])
        else:
            tok = (own, self.cnt[e] + 1)
        waits = []
        wd = self.waited[e]
        for s, v in deps.items():
            if wd.get(s, 0) >= v:
                continue
            wd[s] = v
            waits.append((self.semobj[s], v))
        for vw in reads:
            sp = self.spaces[vw.space]
            for lo, hi in vw.runs:
                for r in sp.carve(lo, hi):
                    if r.r.get(tok[0], 0) < tok[1]:
                        r.r[tok[0]] = tok[1]
        for vw in writes:
            sp = self.spaces[vw.space]
            for lo, hi in vw.runs:
                sp.write_commit(lo, hi, tok)
        if is_out:
            self.out_tokens.append(tok)
        self.n_ops += 1
        self.n_waits += len(waits)
        semh = self.semobj[tok[0]]
        is_dma = dma is not None

        label = self.label
        names = self.names

        def emit(engobj):
            for s, v in waits:
                engobj.wait_ge(s, v)
            ins = fn(engobj)
            if names is not None:
                names.append((e, str(ins.ins.name), label))
            if is_dma:
                ins.then_inc(semh, 16)
            elif count:
                ins.then_inc(semh, 1)

        self.stream[e].append(emit)
        return tok

    def finish(self):
        fin = {}
        for s, v in self.out_tokens:
            if fin.get(s, 0) < v:
                fin[s] = v
        finw = [(self.semobj[s], v) for s, v in fin.items()]

        def fin_emit(engobj):
            for s, v in finw:
                engobj.wait_ge(s, v)

        self.stream['sp'].append(fin_emit)
        nc = self.nc
        st = self.stream
        with nc.Block() as block:
            @block.tensor
            def _(eng):
                for f in st['pe']:
                    f(eng)

            @block.scalar
            def _(eng):
                for f in st['act']:
                    f(eng)

            @block.vector
            def _(eng):
                for f in st['dve']:
                    f(eng)

            @block.gpsimd
            def _(eng):
                for f in st['pool']:
                    f(eng)

            @block.sync
            def _(eng):
                for f in st['sp']:
                    f(eng)


def _norm_idx(ix, n):
    if isinstance(ix, slice):
        a, b, s = ix.indices(n)
        return a, b, s
    return ix, ix + 1, 1


class T:
    def __init__(self, ap, space, off, esz, dims):
        self.ap = ap
        self.space = space
        self.off = off
        self.esz = esz
        self.dims = tuple(dims)

    def __getitem__(self, idx):
        if not isinstance(idx, tuple):
            idx = (idx,)
        idx = list(idx) + [slice(None)] * (1 + len(self.dims) - len(idx))
        ap = self.ap[tuple(idx)]
        fr = [_norm_idx(ix, n) for ix, n in zip(idx[1:], self.dims)]
        strides = []
        acc = 1
        for n in reversed(self.dims):
            strides.append(acc)
            acc *= n
        strides = strides[::-1]
        runs = []

        def rec(d, base):
            a, b, s = fr[d]
            if d == len(fr) - 1:
                runs.append((base + a, base + a + (b - a - 1) // s * s + 1 if b > a else base + a))
                return
            for i in range(a, b, s):
                rec(d + 1, base + i * strides[d])

        rec(0, 0)
        runs.sort()
        merged = []
        for lo, hi in runs:
            if merged and lo <= merged[-1][1]:
                merged[-1][1] = max(merged[-1][1], hi)
            else:
                merged.append([lo, hi])
        bruns = [(self.off + lo * self.esz, self.off + hi * self.esz) for lo, hi in merged]
        return View(ap, self.space, bruns)


class Builder:
    NSLOT = 6
    SLOT_E = 2048
    ARENA_F32 = 52000

    def __init__(self, debug=None, stop_after=None):
        self.debug = debug or {}
        self.stop_after = stop_after
        nc = bass.Bass("TRN2", target_bir_lowering=False)
        self.nc = nc
        self.p = Prog(nc)
        d = {}

        def din(name, shape):
            d[name] = nc.dram_tensor(name, list(shape), F32, kind="ExternalInput").ap()

        def dout(name, shape):
            d[name] = nc.dram_tensor(name, list(shape), F32, kind="ExternalOutput").ap()

        din('xp', [NPR, D]); din('xs', [16, 4, D])
        din('st_pool', [2, 16, 15, 512]); din('st_dnc', [2, 16, 3, 1536])
        din('st_dn', [2, 16, 4, 128, 128]); din('st_cm', [2, 16, 30, 512])
        din('pp', [2, NPR, 256]); din('psm', [2, 16, 4, 256])
        din('w_up1', [2, D, 2 * DFF]); din('w_dn1', [2, DFF, D])
        din('w_in', [2, D, WIN]); din('pool_w', [2, 4, 128, 128])
        din('w_br', [2, 3, 512, D]); din('w_out', [2, D, D])
        din('w_up2', [2, D, 2 * DFF]); din('w_dn2', [2, DFF, D])
        din('w_pg', [2, D, D]); din('w_pp', [2, 256, D])
        din('vecs', [2, 128, NVEC]); din('consts', [128, NCON])
        dout('y_p', [NPR, D]); dout('y_s', [16, 4, D])
        dout('npool_p', [2, 15, 512]); dout('ndnc_p', [2, 3, 1536])
        dout('ndn_p', [2, 4, 128, 128]); dout('ncm_p', [2, 30, 512])
        dout('npool_s', [2, 16, 15, 512]); dout('ndnc_s', [2, 16, 3, 1536])
        dout('ndn_s', [2, 16, 4, 128, 128]); dout('ncm_s', [2, 16, 30, 512])
        self.dbg_out = {}
        for name, shape in self.debug.items():
            self.dbg_out[name] = nc.dram_tensor('dbg_' + name, list(shape), F32, kind="ExternalOutput").ap()
        self.d = d
        self.arena = nc.alloc_sbuf_tensor("arena", [128, self.ARENA_F32], F32)
        self.psb = [nc.alloc_psum_tensor("psb%d" % i, [128, 512], F32) for i in range(8)]
        self.ring_sems = [self.p.new_dma_sem('sem_ring%d' % i) for i in range(self.NSLOT)]
        self.wplan = []

    def alloc(self, dims, dtype=F32):
        esz = 4 if dtype == F32 else 2
        n = int(np.prod(dims))
        nbytes = (n * esz + 31) // 32 * 32
        off = self.top
        self.top += nbytes
        assert self.top <= self.ARENA_F32 * 4, ("arena overflow", self.top)
        self.peak = max(self.peak, self.top)
        ap = self.arena[:, off // 4: (off + nbytes) // 4]
        if dtype == BF16:
            ap = ap.bitcast(BF16)
        ap = ap[:, 0:n]
        if len(dims) == 2:
            ap = ap.rearrange("p (a b) -> p a b", a=dims[0])
        elif len(dims) == 3:
            ap = ap.rearrange("p (a b c) -> p a b c", a=dims[0], b=dims[1])
        return T(ap, 'sb', off, esz, dims)

    def pbank(self):
        b = self.ps_next % 8
        self.ps_next += 1
        return T(self.psb[b], 'ps', b * 2048, 4, (512,))

    def pbank_bf(self):
        b = self.ps_next % 8
        self.ps_next += 1
        return T(self.psb[b][:, :].bitcast(BF16), 'ps', b * 2048, 2, (1024,))

    def mm(self, out, lhsT, rhs, start=True, stop=True, count=True):
        self.p.op('pe', lambda e, o=out.ap, l=lhsT.ap, r=rhs.ap: e.matmul(o, lhsT=l, rhs=r, start=start, stop=stop),
                  reads=[lhsT, rhs], writes=[out], count=count)

    def mmg(self, out, pairs):
        n = len(pairs)
        for i, (l, r) in enumerate(pairs):
            self.mm(out, l, r, start=(i == 0), stop=(i == n - 1), count=(i == n - 1))

    def tr(self, out, in_, ident):
        self.p.op('pe', lambda e, o=out.ap, i=in_.ap, d=ident.ap: e.transpose(out=o, in_=i, identity=d),
                  reads=[in_, ident], writes=[out])

    def act(self, out, in_, func, bias=None, scale=None, eng='act'):
        kw = {}
        rd = [in_]
        if bias is not None:
            if isinstance(bias, View):
                kw['bias'] = bias.ap
                rd.append(bias)
            else:
                kw['bias'] = float(bias)
        if scale is not None:
            if isinstance(scale, View):
                kw['scale'] = scale.ap
                rd.append(scale)
            else:
                kw['scale'] = float(scale)
        self.p.op('act', lambda e, o=out.ap, i=in_.ap: e.activation(out=o, in_=i, func=func, **kw),
                  reads=rd, writes=[out])

    def tcopy(self, eng, out, in_):
        if eng == 'act':
            self.p.op('act', lambda e, o=out.ap, i=in_.ap: e.copy(out=o, in_=i), reads=[in_], writes=[out])
        else:
            self.p.op(eng, lambda e, o=out.ap, i=in_.ap: e.tensor_copy(out=o, in_=i), reads=[in_], writes=[out])

    def tt(self, eng, out, in0, in1, op):
        self.p.op(eng, lambda e, o=out.ap, a=in0.ap, b=in1.ap: e.tensor_tensor(out=o, in0=a, in1=b, op=op),
                  reads=[in0, in1], writes=[out])

    def ts(self, eng, out, in0, s1, op0, s2=None, op1=None):
        rd = [in0]
        a1 = s1
        if isinstance(s1, View):
            rd.append(s1)
            a1 = s1.ap
        a2 = s2
        if isinstance(s2, View):
            rd.append(s2)
            a2 = s2.ap
        kw = {}
        if op1 is not None:
            kw['op1'] = op1
        self.p.op(eng, lambda e, o=out.ap, i=in0.ap: e.tensor_scalar(out=o, in0=i, scalar1=a1, scalar2=a2, op0=op0, **kw),
                  reads=rd, writes=[out])

    def stt(self, out, in0, scalar, in1, op0, op1):
        rd = [in0, in1]
        sc = scalar
        if isinstance(scalar, View):
            rd.append(scalar)
            sc = scalar.ap
        self.p.op('dve', lambda e, o=out.ap, a=in0.ap, b=in1.ap: e.scalar_tensor_tensor(out=o, in0=a, scalar=sc, in1=b, op0=op0, op1=op1),
                  reads=rd, writes=[out])

    def memset(self, eng, out, val):
        self.p.op(eng, lambda e, o=out.ap: e.memset(o, val), writes=[out])

    def dma(self, eng, out, in_, is_out=False, sem='auto'):
        rd = [in_] if isinstance(in_, View) else []
        wr = [out] if isinstance(out, View) else []
        oa = out.ap if isinstance(out, View) else out
        ia = in_.ap if isinstance(in_, View) else in_
        self.p.op(eng, lambda e: e.dma_start(out=oa, in_=ia), reads=rd, writes=wr, dma=sem, is_out=is_out)

    def wget(self, specs):
        if self.p.dry:
            i0 = len(self.wplan)
            self.wplan.extend(specs)
        else:
            i0 = self.wnext
            self.wnext += len(specs)
            lim = min(len(self.wplan), i0 + self.NSLOT)
            while self.wissued < lim:
                j = self.wissued
                ap, A, B = self.wplan[j]
                sl = self.slots[j % self.NSLOT]
                dst = sl[:, 0:A * B]
                dst = View(dst.ap.rearrange("p (a b) -> p a b", a=A), dst.space, dst.runs)
                self.dma('pool', dst, ap, sem=self.ring_sems[j % self.NSLOT])
                self.wissued += 1
        out = []
        for k, (ap, A, B) in enumerate(specs):
            sl = self.slots[(i0 + k) % self.NSLOT]
            out.append(T(sl.ap[:, 0:A * B].rearrange("p (a b) -> p a b", a=A), 'sb', sl.off, 2, (A, B)))
        return out

    def wspec_cols(self, w2d, c0, n, kcn=KC):
        return (w2d[:, c0:c0 + n].rearrange("(kc p) n -> p kc n", p=128), kcn, n)

    def wspec_rows(self, w2d, r0, nch):
        ncol = w2d.shape[1]
        return (w2d[r0 * 128:(r0 + nch) * 128, :].rearrange("(j p) n -> p j n", p=128), nch, ncol)

    @staticmethod
    def ttiles(c0, n, step=512):
        out = []
        c = c0
        while c < c0 + n:
            w = min(step, c0 + n - c)
            out.append((c, w))
            c += w
        return out

    def build(self):
        self.p.dry = True
        self.emit_all()
        self.p.dry = False
        self.emit_all()
        self.p.finish()
        return self.nc

    def emit_all(self):
        self.top = 0
        self.peak = 0
        self.ps_next = 0
        self.wnext = 0
        self.wissued = 0
        self.dbg_done = set()
        d = self.d
        self.xT = self.alloc([KC, NT], F32)
        self.cst = self.alloc([NCON], F32)
        self.vec = self.alloc([DEPTH, NVEC], F32)
        self.idb = self.alloc([128], BF16)
        self.oneb = self.alloc([128], BF16)
        self.slots = [self.alloc([self.SLOT_E], BF16) for _ in range(self.NSLOT)]
        self.tails_pool = self.alloc([4, 15], F32)
        self.tails_dn = self.alloc([12, 3], F32)
        self.tails_cm = self.alloc([4, 30], BF16)
        self.Sst = self.alloc([4, 128], F32)
        self.glc = self.alloc([16], F32)
        self.gls = self.alloc([16], F32)
        self.phase0 = self.top
        self.stage = [self.alloc([D], F32) for _ in range(2)]
        self.dma('sp', self.cst[:, :], d['consts'])
        self.dma('sp', self.vec[:, :, :], d['vecs'].rearrange("l p n -> p l n"))
        self.tcopy('dve', self.idb[:, :], self.cst[:, C_ID:C_ID + 128])
        self.tcopy('dve', self.oneb[:, :], self.cst[:, C_ONES:C_ONES + 128])
        self.ident = lambda n=128: self.cst[0:n, C_ID:C_ID + n]
        self.tap('cst', self.cst)
        if self.stop_after == 'consts':
            return
        self.p.label = 'load'
        self.load_x()
        if self.stop_after == 'loadx':
            self.tap('x0', self.xT)
            return
        self.tap('x0', self.xT)
        if self.stop_after == 'load':
            return self.final()
        for l in range(DEPTH):
            self.p.label = 'ffn1_%d' % l
            self.ffn(l, d['w_up1'][l], d['w_dn1'][l], V_GFFN1)
            self.tap('x1_%d' % l, self.xT)
            if self.stop_after == 'ffn1_%d' % l:
                return self.final()
            self.mixer(l)
            self.tap('x2_%d' % l, self.xT)
            if self.stop_after == 'mix_%d' % l:
                return self.final()
            self.p.label = 'ffn2_%d' % l
            self.ffn(l, d['w_up2'][l], d['w_dn2'][l], V_GFFN2)
            self.tap('x3_%d' % l, self.xT)
            self.p.label = 'ple_%d' % l
            self.ple(l)
            self.tap('x4_%d' % l, self.xT)
            if self.stop_after == 'layer_%d' % l:
                return self.final()
        self.final()

    def tap(self, name, tile, view=None, eng='sp'):
        if name not in self.dbg_out or name in self.dbg_done:
            return
        self.dbg_done.add(name)
        v = view if view is not None else tile[(slice(None),) * (1 + len(tile.dims))]
        self.dma(eng, self.dbg_out[name], v, is_out=True)

    def load_x(self):
        d = self.d
        ev = 0
        for i in range(17):
            st = self.stage[i % 2]
            if i < 16:
                n = 128
                self.dma('sp', st[:, :], d['xp'][i * 128:(i + 1) * 128, :])
                c0 = i * 128
            else:
                n = 64
                for t in range(4):
                    self.dma('sp', st[16 * t:16 * t + 16, :], d['xs'][:, t, :])
                c0 = NPR
            for g in range(2):
                ps = self.pbank()
                for j in range(4):
                    kc = 4 * g + j
                    self.tr(ps[:, j * 128:j * 128 + n], st[0:n, kc * 128:(kc + 1) * 128], self.ident(n))
                for j in range(4):
                    kc = 4 * g + j
                    self.tcopy('act' if ev % 2 else 'dve', self.xT[:, kc, c0:c0 + n], ps[:, j * 128:j * 128 + n])
                    ev += 1

    def rstd_tile(self, srcs, n, scale, eps, bias2=0.0):
        sq = self.tmp_sq
        for k, s in enumerate(srcs):
            self.act(sq[:, k, 0:n], s, AF.Square)
        ps = self.pbank()
        self.mmg(ps[:, 0:n], [(self.oneb[:, :], sq[:, k, 0:n]) for k in range(len(srcs))])
        self.act(self.tmp_ln[:, 0:n], ps[:, 0:n], AF.Ln, bias=self.epsv(eps), scale=scale)
        self.act(self.tmp_rs[:, 0:n], self.tmp_ln[:, 0:n], AF.Exp, scale=-0.5, bias=self.epsv(bias2) if bias2 != 0.0 else None)
        return self.tmp_rs[:, 0:n]

    def rstd_psum(self, sq_views, n, scale, eps, bias2=0.0):
        ps = self.pbank()
        self.mmg(ps[:, 0:n], [(self.oneb[:, :], v) for v in sq_views])
        return ps

    def rstd_psum_fin(self, ps, n, scale, eps, bias2=0.0, stage=0):
        if stage == 0:
            self.act(ps[:, 0:n], ps[:, 0:n], AF.Ln, bias=self.epsv(eps), scale=scale)
        else:
            self.act(ps[:, 0:n], ps[:, 0:n], AF.Exp, scale=-0.5, bias=self.epsv(bias2) if bias2 != 0.0 else None)

    def epsv(self, val, p=128):
        key = float(val)
        if key not in self.eps_cols:
            idx = len(self.eps_cols)
            assert idx < 8
            self.eps_cols[key] = idx
            self.memset('dve', self.epst[:, idx:idx + 1], key)
        i = self.eps_cols[key]
        return self.epst[0:p, i:i + 1]

    def norm_tmps(self, nsq=KC, small=False):
        self.tmp_sq = self.alloc([nsq, 512], BF16)
        if not small:
            self.tmp_ln = self.alloc([512], F32)
            self.tmp_rs = self.alloc([512], F32)
        self.epst = self.alloc([8], F32)
        self.eps_cols = {}

    def rmsnorm_full(self, l, vcol, u, c0=0, ncols=NT, ucol0=0):
        for (c, n) in self.ttiles(c0, ncols):
            rs = self.rstd_tile([self.xT[:, kc, c:c + n] for kc in range(KC)], n, 1.0 / D, RMS_EPS)
            for kc in range(KC):
                self.stt(u[:, kc, ucol0 + c - c0:ucol0 + c - c0 + n], self.xT[:, kc, c:c + n],
                         self.vec[:, l, vcol + kc:vcol + kc + 1], rs, ALU.mult, ALU.mult)

    def final(self):
        d = self.d
        self.p.label = 'final'
        self.top = self.phase0
        self.stage = [self.alloc([D], F32) for _ in range(2)]
        self.norm_tmps()
        yt = self.alloc([KC, 128], F32)
        ev = 0
        for i in range(17):
            n = 128 if i < 16 else 64
            c0 = i * 128
            rs = self.rstd_tile([self.xT[:, kc, c0:c0 + n] for kc in range(KC)], n, 1.0 / D, RMS_EPS)
            for kc in range(KC):
                self.stt(yt[:, kc, 0:n], self.xT[:, kc, c0:c0 + n], self.vec[:, 0, V_GFIN + kc:V_GFIN + kc + 1], rs,
                         ALU.mult, ALU.mult)
            st = self.stage[i % 2]
            for g in range(2):
                ps = self.pbank()
                for j in range(4):
                    kc = 4 * g + j
                    self.tr(ps[0:n, j * 128:(j + 1) * 128], yt[:, kc, 0:n], self.ident(128))
                self.tcopy('act' if ev % 2 else 'dve', st[0:n, g * 512:(g + 1) * 512], ps[0:n, :])
                ev += 1
            if i < 16:
                self.dma('sp', d['y_p'][i * 128:(i + 1) * 128, :], st[:, :], is_out=True)
            else:
                for t in range(4):
                    self.dma('sp', d['y_s'][:, t, :], st[16 * t:16 * t + 16, :], is_out=True)

    def ffn(self, l, w_up, w_dn, vcol):
        self.top = self.phase0
        self.norm_tmps()
        u = self.alloc([KC, NT], BF16)
        h = self.alloc([6, NT], BF16)
        sa = [self.alloc([512], BF16) for _ in range(2)]
        self.rmsnorm_full(l, vcol, u)
        tts = self.ttiles(0, NT)
        groups = [(0, 6), (6, 6), (12, 6), (18, 4)]
        ev = 0
        for (j0, nj) in groups:
            for q in range(j0 // 2, (j0 + nj) // 2):
                wa, wb = self.wget([self.wspec_cols(w_up, 256 * q, 256), self.wspec_cols(w_up, DFF + 256 * q, 256)])
                for jj in range(2):
                    jl = 2 * q + jj - j0
                    for (c, n) in tts:
                        pa = self.pbank()
                        pb = self.pbank()
                        self.mmg(pa[:, 0:n], [(wa[:, kc, jj * 128:(jj + 1) * 128], u[:, kc, c:c + n]) for kc in range(KC)])
                        self.mmg(pb[:, 0:n], [(wb[:, kc, jj * 128:(jj + 1) * 128], u[:, kc, c:c + n]) for kc in range(KC)])
                        s = sa[ev % 2]
                        ev += 1
                        self.act(s[:, 0:n], pa[:, 0:n], AF.Silu)
                        self.tt('dve', h[:, jl, c:c + n], pb[:, 0:n], s[:, 0:n], ALU.mult)
            wd = self.wget([self.wspec_rows(w_dn, j0 + 2 * i, 2) for i in range(nj // 2)])
            for (c, n) in tts:
                for m in range(KC):
                    py = self.pbank()
                    self.mmg(py[:, 0:n], [(wd[j // 2][:, j % 2, m * 128:(m + 1) * 128], h[:, j, c:c + n]) for j in range(nj)])
                    self.stt(self.xT[:, m, c:c + n], py[:, 0:n], 0.5, self.xT[:, m, c:c + n], ALU.mult, ALU.add)

    def ple(self, l):
        d = self.d
        self.top = self.phase0
        self.norm_tmps()
        u = self.alloc([KC, NT], BF16)
        pT = self.alloc([2, NT], BF16)
        sg = [self.alloc([512], F32) for _ in range(2)]
        pr = [self.alloc([512], F32) for _ in range(2)]
        pst = [self.alloc([256], F32) for _ in range(2)]
        for i in range(17):
            st = pst[i % 2]
            if i < 16:
                n = 128
                self.dma('sp', st[:, :], d['pp'][l, i * 128:(i + 1) * 128, :])
            else:
                n = 64
                for t in range(4):
                    self.dma('sp', st[16 * t:16 * t + 16, :], d['psm'][l, :, t, :])
            ps = self.pbank()
            for j in range(2):
                self.tr(ps[:, j * 128:j * 128 + n], st[0:n, j * 128:(j + 1) * 128], self.ident(n))
            for j in range(2):
                self.tcopy('act' if j else 'dve', pT[:, j, i * 128:i * 128 + n], ps[:, j * 128:j * 128 + n])
        self.rmsnorm_full(l, V_GPLE, u)
        tts = self.ttiles(0, NT)
        ev = 0
        for q in range(4):
            wg, wp = self.wget([self.wspec_cols(d['w_pg'][l], 256 * q, 256),
                                self.wspec_cols(d['w_pp'][l], 256 * q, 256, kcn=2)])
            for jj in range(2):
                m = 2 * q + jj
                for (c, n) in tts:
                    pg = self.pbank()
                    pe = self.pbank()
                    self.mmg(pg[:, 0:n], [(wg[:, kc, jj * 128:(jj + 1) * 128], u[:, kc, c:c + n]) for kc in range(KC)])
                    self.mmg(pe[:, 0:n], [(wp[:, kc, jj * 128:(jj + 1) * 128], pT[:, kc, c:c + n]) for kc in range(2)])
                    s = sg[ev % 2]
                    r = pr[ev % 2]
                    ev += 1
                    self.act(s[:, 0:n], pg[:, 0:n], AF.Sigmoid)
                    self.tt('dve', r[:, 0:n], pe[:, 0:n], s[:, 0:n], ALU.mult)
                    self.tt('pool', self.xT[:, m, c:c + n], self.xT[:, m, c:c + n], r[:, 0:n], ALU.add)

    def mixer(self, l):
        for pi in range(2):
            self.mixer_pass(l, pi)

    def mixer_pass(self, l, pi):
        d = self.d
        Lp = 1024
        has_s = (pi == 1)
        W = Lp + (NSM if has_s else 0)
        xc0 = pi * Lp
        w_in = d['w_in'][l]
        self.top = self.phase0
        u = self.alloc([KC, W], BF16)
        o_dn = self.alloc([4, W], BF16)
        mark0 = self.top
        self.norm_tmps()
        self.p.label = 'mixnorm_%d_%d' % (l, pi)
        self.rmsnorm_full(l, V_GMIX, u, c0=xc0, ncols=W, ucol0=0)
        tiles = [(0, 512, 'p'), (512, 512, 'p')] + ([(Lp, NSM, 's')] if has_s else [])
        if has_s:
            self.dma('act', d['npool_s'][l][:, 0:11, :], d['st_pool'][l][:, 4:15, :], is_out=True)
            self.dma('act', d['ncm_s'][l][:, 0:26, :], d['st_cm'][l][:, 4:30, :], is_out=True)
        self.top = mark0
        self.norm_tmps(nsq=1, small=True)
        self.p.label = 'dn_%d_%d' % (l, pi)
        self.dn_branch(l, pi, u, o_dn, tiles, Lp, has_s, W)
        self.tap('odn_%d_%d' % (l, pi), o_dn, eng='pool')
        self.top = mark0
        o_pool = self.alloc([4, W], BF16)
        o_cm = self.alloc([4, W], BF16)
        mark1 = self.top
        self.p.label = 'pool_%d_%d' % (l, pi)
        self.pool_branch(l, pi, u, o_pool, tiles, Lp, has_s, W)
        self.tap('opool_%d_%d' % (l, pi), o_pool, eng='pool')
        self.top = mark1
        self.norm_tmps(nsq=4)
        self.p.label = 'cm_%d_%d' % (l, pi)
        self.cm_branch(l, pi, u, o_cm, tiles, Lp, has_s, W)
        self.tap('ocm_%d_%d' % (l, pi), o_cm, eng='pool')
        self.top = mark1
        self.p.label = 'merge_%d_%d' % (l, pi)
        mg = self.alloc([KC, W], BF16)
        sg = [self.alloc([512], F32) for _ in range(2)]
        pr = [self.alloc([512], F32) for _ in range(2)]
        acc = self.alloc([len(tiles), 512], F32)
        obr = [o_pool, o_dn, o_cm]
        ev = 0
        for m in range(KC):
            for nb in range(3):
                wg_, wb_ = self.wget([self.wspec_cols(w_in, 3592 + nb * 1024 + 128 * m, 128),
                                      self.wspec_cols(d['w_br'][l][nb], 128 * m, 128, kcn=4)])
                for ti, (c, n, kind) in enumerate(tiles):
                    pg = self.pbank()
                    pb = self.pbank()
                    self.mmg(pg[:, 0:n], [(wg_[:, kc, :], u[:, kc, c:c + n]) for kc in range(KC)])
                    self.mmg(pb[:, 0:n], [(wb_[:, kc, :], obr[nb][:, kc, c:c + n]) for kc in range(4)])
                    sgt = sg[ev % 2]
                    prt = pr[ev % 2]
                    ev += 1
                    self.act(sgt[:, 0:n], pg[:, 0:n], AF.Sigmoid)
                    if nb == 0:
                        self.tt('dve', acc[:, ti, 0:n], pb[:, 0:n], sgt[:, 0:n], ALU.mult)
                    else:
                        self.tt('dve', prt[:, 0:n], pb[:, 0:n], sgt[:, 0:n], ALU.mult)
                        if nb == 1:
                            self.tt('pool', acc[:, ti, 0:n], acc[:, ti, 0:n], prt[:, 0:n], ALU.add)
                        else:
                            self.tt('pool', mg[:, m, c:c + n], acc[:, ti, 0:n], prt[:, 0:n], ALU.add)
        self.tap('mg_%d_%d' % (l, pi), mg, eng='pool')
        for q in range(4):
            (wo,) = self.wget([self.wspec_cols(d['w_out'][l], 256 * q, 256)])
            for jj in range(2):
                m = 2 * q + jj
                for (c, n, kind) in tiles:
                    py = self.pbank()
                    self.mmg(py[:, 0:n], [(wo[:, kc, jj * 128:(jj + 1) * 128], mg[:, kc, c:c + n]) for kc in range(KC)])
                    self.tt('dve', self.xT[:, m, xc0 + c:xc0 + c + n], py[:, 0:n], self.xT[:, m, xc0 + c:xc0 + c + n], ALU.add)

    def pool_branch(self, l, pi, u, o_pool, tiles, Lp, has_s, W):
        d = self.d
        w_in = d['w_in'][l]
        E = 15 + Lp
        xp = self.alloc([E], F32)
        A = self.alloc([E], F32)
        Bb = self.alloc([E], F32)
        dT = self.alloc([W], BF16)
        t16 = self.alloc([16], F32)
        if has_s:
            ES = 19 * 16
            xps = self.alloc([ES], F32)
            As = self.alloc([ES], F32)
            Bs = self.alloc([ES], F32)
            stg = [self.alloc([512], F32) for _ in range(2)]
            stS = self.alloc([512], F32)
            for t in range(15):
                self.dma('sp', stg[t // 8][16 * (t % 8):16 * (t % 8) + 16, :], d['st_pool'][l][:, t, :])
        (pwr,) = self.wget([(d['pool_w'][l].rearrange("g c e -> c g e"), 4, 128)])
        pw = self.alloc([4, 128], BF16)
        self.tcopy('pool', pw[:, :, :], pwr[:, :, :])
        for gi in range(4):
            wsz = (2, 4, 8, 16)[gi]
            (wp,) = self.wget([self.wspec_cols(w_in, gi * 128, 128)])
            if pi == 0:
                self.memset('pool', xp[:, 0:15], 0.0)
            else:
                self.tcopy('pool', xp[:, 0:15], self.tails_pool[:, gi, :])
            for (c, n, kind) in tiles:
                ps = self.pbank()
                self.mmg(ps[:, 0:n], [(wp[:, kc, :], u[:, kc, c:c + n]) for kc in range(KC)])
                if kind == 'p':
                    self.tcopy('act', xp[:, 15 + c:15 + c + n], ps[:, 0:n])
                else:
                    self.tcopy('act', xps[:, 240:304], ps[:, 0:n])
            self.tcopy('pool', self.tails_pool[:, gi, :], xp[:, Lp:Lp + 15])
            if has_s:
                ps = self.pbank()
                self.tr(ps[:, 0:128], stg[0][0:128, gi * 128:(gi + 1) * 128], self.ident(128))
                self.tr(ps[:, 128:240], stg[1][0:112, gi * 128:(gi + 1) * 128], self.ident(112))
                self.tcopy('dve', xps[:, 0:240], ps[:, 0:240])
                ps = self.pbank()
                self.tr(ps[0:64, 0:128], xps[:, 240:304], self.ident(128))
                self.tcopy('act', stS[0:64, gi * 128:(gi + 1) * 128], ps[0:64, 0:128])
            src, srcs = xp, (xps if has_s else None)
            bufs, bufss = [A, Bb], ([As, Bs] if has_s else None)
            for k in range(1, gi + 2):
                dst = bufs[(k - 1) % 2]
                i0 = 2 ** k - 1
                sh = 2 ** (k - 1)
                self.tt('dve', dst[:, i0:E], src[:, i0:E], src[:, i0 - sh:E - sh], ALU.add)
                src = dst
                if has_s:
                    dsts = bufss[(k - 1) % 2]
                    self.tt('pool', dsts[:, 16 * i0:ES], srcs[:, 16 * i0:ES], srcs[:, 16 * (i0 - sh):ES - 16 * sh], ALU.add)
                    srcs = dsts
            invw = 1.0 / wsz
            self.stt(dT[:, 0:Lp], src[:, 15:15 + Lp], invw, xp[:, 15:15 + Lp], ALU.mult, ALU.subtract)
            if pi == 0:
                self.tt('dve', t16[:, :], src[:, 15:31], self.cst[:, C_INVCNT + gi * 16:C_INVCNT + gi * 16 + 16], ALU.mult)
                self.tt('dve', dT[:, 0:16], t16[:, :], xp[:, 15:31], ALU.subtract)
            if has_s:
                self.stt(dT[:, Lp:Lp + 64], srcs[:, 240:304], invw, xps[:, 240:304], ALU.mult, ALU.subtract)
            if gi == 3 and pi == 0:
                self.tap('dbgxp', xp); self.tap('dbgA', A); self.tap('dbgB', Bb); self.tap('dbgdT', dT, eng='pool')
            for (c, n, kind) in tiles:
                ps = self.pbank()
                self.mm(ps[:, 0:n], pw[:, gi, :], dT[:, c:c + n])
                self.ts('dve', o_pool[:, gi, c:c + n], ps[:, 0:n], self.vec[:, l, V_PSCALE + gi:V_PSCALE + gi + 1], ALU.mult)
        if pi == 1:
            ps = self.pbank()
            for gi in range(4):
                self.tr(ps[0:15, gi * 128:(gi + 1) * 128], self.tails_pool[:, gi, :], self.ident(128))
            sto = self.alloc([512], F32)
            self.tcopy('dve', sto[0:15, :], ps[0:15, :])
            self.dma('sp', d['npool_p'][l], sto[0:15, :], is_out=True)
            for t in range(4):
                self.dma('sp', d['npool_s'][l][:, 11 + t, :], stS[16 * t:16 * t + 16, :], is_out=True)

    def cm_branch(self, l, pi, u, o_cm, tiles, Lp, has_s, W):
        d = self.d
        w_in = d['w_in'][l]
        E = 30 + Lp
        hb = self.alloc([4, E], BF16)
        dg = self.alloc([31, 128], BF16)
        sig = self.alloc([512], F32)
        if has_s:
            ES = 34 * 16
            hbs = self.alloc([4, ES], BF16)
            stS = self.alloc([512], F32)
            hs32 = self.alloc([4, 64], F32)
        cm_mark = self.top
        if has_s:
            stg = [self.alloc([512], F32) for _ in range(4)]
            for t in range(30):
                self.dma('sp' if t % 2 else 'act', stg[t // 8][16 * (t % 8):16 * (t % 8) + 16, :], d['st_cm'][l][:, t, :])
        for c4 in range(4):
            wa, wg = self.wget([self.wspec_cols(w_in, 2568 + c4 * 128, 128), self.wspec_cols(w_in, 2568 + 512 + c4 * 128, 128)])
            if pi == 0:
                self.memset('pool', hb[:, c4, 0:30], 0.0)
            else:
                self.tcopy('pool', hb[:, c4, 0:30], self.tails_cm[:, c4, :])
            for (c, n, kind) in tiles:
                pa = self.pbank()
                pg = self.pbank()
                self.mmg(pa[:, 0:n], [(wa[:, kc, :], u[:, kc, c:c + n]) for kc in range(KC)])
                self.mmg(pg[:, 0:n], [(wg[:, kc, :], u[:, kc, c:c + n]) for kc in range(KC)])
                self.act(sig[:, 0:n], pg[:, 0:n], AF.Sigmoid)
                if kind == 'p':
                    self.tt('dve', hb[:, c4, 30 + c:30 + c + n], pa[:, 0:n], sig[:, 0:n], ALU.mult)
                else:
                    self.tt('dve', hs32[:, c4, 0:64], pa[:, 0:n], sig[:, 0:n], ALU.mult)
                    self.tcopy('pool', hbs[:, c4, 480:544], hs32[:, c4, 0:64])
            self.tcopy('pool', self.tails_cm[:, c4, :], hb[:, c4, Lp:Lp + 30])
        for c4 in range(4):
            if has_s:
                for blk in range(4):
                    nr = 128 if blk < 3 else 96
                    ps = self.pbank()
                    self.tr(ps[:, 0:nr], stg[blk][0:nr, c4 * 128:(c4 + 1) * 128], self.ident(nr))
                    self.tcopy('dve', hbs[:, c4, blk * 128:blk * 128 + nr], ps[:, 0:nr])
                ps = self.pbank()
                self.tr(ps[0:64, 0:128], hs32[:, c4, 0:64], self.ident(128))
                self.tcopy('act', stS[0:64, c4 * 128:(c4 + 1) * 128], ps[0:64, 0:128])
        self.top = cm_mark
        yall = self.alloc([4, W], F32)
        for c4 in range(4):
            for k in range(31):
                self.ts('dve', dg[:, k, :], self.idb[:, :],
                        self.vec[:, l, V_CMW + c4 * 31 + k:V_CMW + c4 * 31 + k + 1], ALU.mult)
            for (c, n, kind) in tiles:
                ps = self.pbank()
                if kind == 'p':
                    self.mmg(ps[:, 0:n], [(dg[:, k, :], hb[:, c4, c + k:c + k + n]) for k in range(31)])
                else:
                    self.mmg(ps[:, 0:n], [(dg[:, k, :], hbs[:, c4, 16 * k:16 * k + 64]) for k in range(31)])
                self.act(yall[:, c4, c:c + n], ps[:, 0:n], AF.Identity, bias=self.vec[:, l, V_CMB + c4:V_CMB + c4 + 1])
        ones32 = self.cst[:, C_ONES:C_ONES + 128]
        yn = self.alloc([512], F32)
        for (c, n, kind) in tiles:
            ps = self.pbank()
            self.mmg(ps[:, 0:n], [(ones32, yall[:, c4, c:c + n]) for c4 in range(4)])
            mu = sig
            self.act(mu[:, 0:n], ps[:, 0:n], AF.Copy, scale=-1.0 / 512)
            for c4 in range(4):
                self.tt('pool', yall[:, c4, c:c + n], yall[:, c4, c:c + n], mu[:, 0:n], ALU.add)
            rs = self.rstd_tile([yall[:, c4, c:c + n] for c4 in range(4)], n, 1.0 / 512, LN_EPS)
            for c4 in range(4):
                self.stt(yn[:, 0:n], yall[:, c4, c:c + n], self.vec[:, l, V_LNG + c4:V_LNG + c4 + 1], rs, ALU.mult, ALU.mult)
                self.act(o_cm[:, c4, c:c + n], yn[:, 0:n], AF.Silu, bias=self.vec[:, l, V_LNB + c4:V_LNB + c4 + 1])
        if pi == 1:
            t32 = self.alloc([4, 32], F32)
            self.tcopy('dve', t32[:, :, 0:30], self.tails_cm[:, :, :])
            ps = self.pbank()
            for c4 in range(4):
                self.tr(ps[0:30, c4 * 128:(c4 + 1) * 128], t32[:, c4, 0:30], self.ident(128))
            sto = self.alloc([512], F32)
            self.tcopy('dve', sto[0:30, :], ps[0:30, :])
            self.dma('sp', d['ncm_p'][l], sto[0:30, :], is_out=True)
            for t in range(4):
                self.dma('sp', d['ncm_s'][l][:, 26 + t, :], stS[16 * t:16 * t + 16, :], is_out=True)

    def dn_branch(self, l, pi, u, o_dn, tiles, Lp, has_s, W):
        d = self.d
        w_in = d['w_in'][l]
        cst = self.cst
        chunks = [(i * 128, 128, 'p') for i in range(Lp // 128)] + ([(Lp, NSM, 's')] if has_s else [])
        nch = len(chunks)
        sg_all = self.alloc([W], F32)
        gcT = self.alloc([W], F32)
        toks = self.alloc([nch, 8], F32)
        gct = self.alloc([nch, 4], F32)
        glt = self.alloc([nch, 4], F32)
        Gt = self.alloc([nch, 4], F32)
        ktt = self.alloc([nch, 4], F32)
        nbg = self.alloc([nch, 4], F32)
        tmp4 = self.alloc([4], F32)
        negA = self.alloc([8], F32)
        dselA = self.alloc([8], F32)
        mark = self.top
        sp_all = self.alloc([W], F32)
        e1 = self.alloc([512], F32)
        self.act(negA[0:8, 0:1], self.vec[0:8, l, V_ALOG:V_ALOG + 1], AF.Exp)
        self.ts('dve', dselA[0:8, 0:8], cst[0:8, C_DSELA:C_DSELA + 8], negA[0:8, 0:1], ALU.mult, -1.0, ALU.mult)
        (wab,) = self.wget([self.wspec_cols(w_in, 2560, 8)])
        for (c, n, kind) in tiles:
            ps = self.pbank()
            self.mmg(ps[0:8, 0:n], [(wab[:, kc, 0:8], u[:, kc, c:c + n]) for kc in range(KC)])
            self.act(e1[0:8, 0:n], ps[0:8, 0:n], AF.Exp, bias=self.vec[0:8, l, V_DTB:V_DTB + 1])
            self.act(sp_all[0:8, c:c + n], e1[0:8, 0:n], AF.Ln, bias=self.epsv(1.0, 8))
            self.act(sg_all[0:8, c:c + n], ps[0:8, 0:n], AF.Sigmoid)
        for ci, (c, n, kind) in enumerate(chunks):
            ps = self.pbank()
            self.mm(ps[0:n, 0:8], sp_all[0:8, c:c + n], dselA[0:8, 0:8], start=True, stop=False, count=False)
            self.mm(ps[0:n, 0:8], sg_all[0:8, c:c + n], cst[0:8, C_DSELB:C_DSELB + 8], start=False, stop=True)
            self.tcopy('dve', toks[0:n, ci, :], ps[0:n, 0:8])
            m1 = cst[0:n, C_M1:C_M1 + n] if kind == 'p' else cst[0:n, C_M1S:C_M1S + n]
            al = cst[0:n, C_ONES:C_ONES + n] if kind == 'p' else cst[0:n, C_ALLS:C_ALLS + n]
            ps2 = self.pbank()
            self.mm(ps2[0:n, 0:4], m1, toks[0:n, ci, 0:4])
            self.mm(ps2[0:n, 4:8], al, toks[0:n, ci, 0:4])
            self.tcopy('dve', gct[0:n, ci, :], ps2[0:n, 0:4])
            self.tcopy('dve', glt[0:n, ci, :], ps2[0:n, 4:8])
            ps3 = self.pbank()
            self.mm(ps3[0:8, 0:n], toks[0:n, ci, 0:8], m1)
            self.tcopy('act', gcT[0:8, c:c + n], ps3[0:8, 0:n])
            self.act(Gt[0:n, ci, :], gct[0:n, ci, :], AF.Exp)
            self.tt('dve', tmp4[0:n, :], glt[0:n, ci, :], gct[0:n, ci, :], ALU.subtract)
            self.act(ktt[0:n, ci, :], tmp4[0:n, :], AF.Exp)
            self.stt(nbg[0:n, ci, :], toks[0:n, ci, 4:8], -1.0, Gt[0:n, ci, :], ALU.mult, ALU.mult)
        self.top = mark
        qT = self.alloc([W], BF16)
        kT = self.alloc([W], BF16)
        vT = self.alloc([W], BF16)
        kbT = self.alloc([W], BF16)
        qdT = self.alloc([W], BF16)
        gcb = self.alloc([W], F32)
        o_raw = vT
        Ktl = self.alloc([nch, 128], BF16)
        VB = self.alloc([nch, 128], BF16)
        AttnT = self.alloc([nch, 128], BF16)
        Yf = self.alloc([nch, 128], BF16)
        Rr = self.alloc([128], BF16)
        vn = self.alloc([128], BF16)
        Sbf = self.alloc([128], BF16)
        if has_s:
            stq = self.alloc([3, 128], F32)
            stqo = self.alloc([3, 128], F32)
            sq48 = self.alloc([3, 48], F32)
            Ssb = self.alloc([16, 128], BF16)
            Ssf2 = [self.alloc([4, 128], F32) for _ in range(2)]
            Km = self.alloc([2, 128], BF16)
            Rrs = self.alloc([128], BF16)
            vns = self.alloc([128], BF16)
            self.memset('pool', Km[:, :, :], 0.0)
            self.memset('pool', vns[:, :], 0.0)
            KSTs = self.alloc([64], F32)
            otmp = self.alloc([64], F32)
        umark = self.top
        dg = self.alloc([12, 128], BF16)
        slf = self.alloc([512], F32)
        Gbt = self.alloc([512], F32)
        slf2 = Gbt
        pre3 = self.alloc([3, 3 + Lp], BF16)
        sqb = self.alloc([2, W], BF16)
        if has_s:
            pres3 = self.alloc([3, 112], BF16)
        atop = self.top
        self.top = umark
        GMAX = 4
        ABZ = [[self.alloc([3, 128], F32) for _ in range(2)] for _ in range(GMAX)]
        decTg = [self.alloc([128], F32) for _ in range(GMAX)]
        self.top = max(self.top, atop)
        ev_cnt = [0]

        def evac(out, in_):
            ev_cnt[0] += 1
            self.tcopy('act' if ev_cnt[0] % 2 else 'dve', out, in_)

        for h in range(4):
            ws = self.wget([self.wspec_cols(w_in, 512 + idx * 512 + h * 128, 128) for idx in range(3)])
            if has_s:
                for t in range(3):
                    src = d['st_dnc'][l][:, t, :].rearrange("s (i hh dd) -> s i hh dd", i=3, hh=4)[:, :, h, :]
                    self.dma('sp', stq[16 * t:16 * t + 16, :, :], src)
                self.dma('pool', Ssb[:, :, :], d['st_dn'][l][:, h].rearrange("s k v -> k s v"))
            for idx in range(3):
                for k in range(4):
                    col = V_DNW + (4 * idx + h) * 4 + k
                    self.ts('dve', dg[:, idx * 4 + k, :], self.idb[:, :], self.vec[:, l, col:col + 1], ALU.mult)
            for idx in range(3):
                ch = 4 * idx + h
                if pi == 0:
                    self.memset('pool', pre3[:, idx, 0:3], 0.0)
                else:
                    self.tcopy('pool', pre3[:, idx, 0:3], self.tails_dn[:, ch, :])
                for (c, n, kind) in tiles:
                    ps = self.pbank()
                    self.mmg(ps[:, 0:n], [(ws[idx][:, kc, :], u[:, kc, c:c + n]) for kc in range(KC)])
                    if kind == 'p':
                        self.tcopy('act', pre3[:, idx, 3 + c:3 + c + n], ps[:, 0:n])
                        if c + n == Lp:
                            self.tcopy('dve', self.tails_dn[:, ch, :], ps[:, n - 3:n])
                    else:
                        self.tcopy('act', pres3[:, idx, 48:112], ps[:, 0:n])
                        self.tcopy('dve', sq48[:, idx, :], ps[:, 16:64])
            if has_s:
                ps = self.pbank()
                for idx in range(3):
                    self.tr(ps[:, idx * 48:idx * 48 + 48], stq[0:48, idx, :], self.ident(48))
                for idx in range(3):
                    self.tcopy('dve', pres3[:, idx, 0:48], ps[:, idx * 48:idx * 48 + 48])
            dstT = [qT, kT, vT]
            for idx in range(3):
                for (c, n, kind) in tiles:
                    ps = self.pbank()
                    if kind == 'p':
                        self.mmg(ps[:, 0:n], [(dg[:, idx * 4 + k, :], pre3[:, idx, c + k:c + k + n]) for k in range(4)])
                    else:
                        self.mmg(ps[:, 0:n], [(dg[:, idx * 4 + k, :], pres3[:, idx, 16 * k:16 * k + 64]) for k in range(4)])
                    self.act(dstT[idx][:, c:c + n], ps[:, 0:n], AF.Silu)
            for idx in range(2):
                for (c, n, kind) in tiles:
                    self.act(sqb[:, idx, c:c + n], dstT[idx][:, c:c + n], AF.Square)
            pss = []
            for idx in range(2):
                for (c, n, kind) in tiles:
                    pss.append((idx, c, n, self.rstd_psum([sqb[:, idx, c:c + n]], n, 1.0, 1e-6)))
            for (idx, c, n, ps) in pss:
                self.rstd_psum_fin(ps, n, 1.0, 1e-6, stage=0)
            for (idx, c, n, ps) in pss:
                self.rstd_psum_fin(ps, n, 1.0, 1e-6, bias2=(-0.5 * float(np.log(128.0)) if idx == 0 else 0.0), stage=1)
            for (idx, c, n, ps) in pss:
                self.tt('dve', dstT[idx][:, c:c + n], dstT[idx][:, c:c + n], ps[:, 0:n], ALU.mult)
            if has_s:
                ps = self.pbank()
                for idx in range(3):
                    self.tr(ps[0:48, idx * 128:(idx + 1) * 128], sq48[:, idx, :], self.ident(128))
                for idx in range(3):
                    self.tcopy('dve', stqo[0:48, idx, :], ps[0:48, idx * 128:(idx + 1) * 128])
                for t in range(3):
                    dst = d['ndnc_s'][l][:, t, :].rearrange("s (i hh dd) -> s i hh dd", i=3, hh=4)[:, :, h, :]
                    self.dma('sp', dst, stqo[16 * t:16 * t + 16, :, :], is_out=True)
            ci = 0
            for (c, n, kind) in tiles:
                ps = self.pbank()
                self.mm(ps[:, 0:n], cst[0:8, C_SEL + h * 128:C_SEL + (h + 1) * 128], gcT[0:8, c:c + n])
                self.tcopy('dve', gcb[:, c:c + n], ps[:, 0:n])
                self.act(Gbt[:, 0:n], ps[:, 0:n], AF.Exp)
                ps2 = self.pbank()
                self.mm(ps2[:, 0:n], cst[0:8, C_SEL + (4 + h) * 128:C_SEL + (5 + h) * 128], sg_all[0:8, c:c + n])
                self.tt('dve', kbT[:, c:c + n], kT[:, c:c + n], ps2[:, 0:n], ALU.mult)
                self.tt('pool', qdT[:, c:c + n], qT[:, c:c + n], Gbt[:, 0:n], ALU.mult)
                if kind == 'p':
                    for j in range(n // 128):
                        self.tcopy('pool', self.glc[:, ci:ci + 1], Gbt[:, j * 128 + 127:j * 128 + 128])
                        ci += 1
                else:
                    self.tcopy('pool', self.gls[:, 0:16], Gbt[:, 48:64])
            if pi == 0:
                self.memset('pool', self.Sst[:, h, :], 0.0)
            self.tcopy('act', Sbf[:, :], self.Sst[:, h, :])

            def pre_steps(ci, gslot):
                c, n, kind = chunks[ci]
                negm = cst[0:n, C_NEGI:C_NEGI + n] if kind == 'p' else cst[0:n, C_NEGIS:C_NEGIS + n]
                strm = cst[0:n, C_STRICT:C_STRICT + n] if kind == 'p' else cst[0:n, C_STRICTS:C_STRICTS + n]
                nlev = 6 if n == 128 else 1
                dT_ = decTg[gslot]
                tl = ABZ[gslot]
                steps = []

                def s0a():
                    psb = self.pbank_bf()
                    self.tr(psb[0:n, 0:128], kT[:, c:c + n], self.idb[:, :])
                    self.tr(psb[0:n, 128:256], vT[:, c:c + n], self.idb[:, :])
                    self.ts('dve', Ktl[0:n, ci, :], psb[0:n, 0:128], ktt[0:n, ci, h:h + 1], ALU.mult)
                    self.ts('dve', VB[0:n, ci, :], psb[0:n, 128:256], toks[0:n, ci, 4 + h:5 + h], ALU.mult)
                    self.stt(dT_[0:n, 0:n], gcb[0:n, c:c + n], gct[0:n, ci, h:h + 1], negm, ALU.subtract, ALU.add)
                    self.act(dT_[0:n, 0:n], dT_[0:n, 0:n], AF.Exp)
                steps.append(s0a)

                def s0b():
                    psq = self.pbank()
                    self.mm(psq[0:n, 0:n], kT[:, c:c + n], qT[:, c:c + n])
                    self.mm(psq[0:n, 128:128 + n], kT[:, c:c + n], kbT[:, c:c + n])
                    self.tt('dve', AttnT[0:n, ci, 0:n], psq[0:n, 0:n], dT_[0:n, 0:n], ALU.mult)
                    self.stt(tl[0][0:n, 1, 0:n], psq[0:n, 128:128 + n], -1.0, dT_[0:n, 0:n], ALU.mult, ALU.mult)
                    self.tt('pool', tl[0][0:n, 1, 0:n], tl[0][0:n, 1, 0:n], strm, ALU.mult)
                    self.tcopy('pool', tl[0][0:n, 2, 0:n], cst[0:n, C_ID:C_ID + n])
                steps.append(s0b)

                def s2():
                    psa = self.pbank()
                    self.tr(psa[0:n, 0:n], tl[0][0:n, 1, 0:n], self.ident(n))
                    evac(tl[0][0:n, 0, 0:n], psa[0:n, 0:n])
                steps.append(s2)

                def mk(lev):
                    def st():
                        src = tl[(lev - 1) % 2]
                        last = (lev == nlev + 1)
                        dstt = tl[lev % 2]
                        ps = self.pbank()
                        A_, B_, Z_ = src[0:n, 0, 0:n], src[0:n, 1, 0:n], src[0:n, 2, 0:n]
                        if not last:
                            self.mm(ps[0:n, 0:n], B_, A_)
                            if lev < nlev:
                                self.mm(ps[0:n, 128:128 + n], A_, B_)
                        self.mm(ps[0:n, 256:256 + n], A_, Z_)
                        if last:
                            self.tt('dve', Yf[0:n, ci, 0:n], ps[0:n, 256:256 + n], Z_, ALU.add)
                        else:
                            self.tcopy('act', dstt[0:n, 0, 0:n], ps[0:n, 0:n])
                            if lev < nlev:
                                self.tcopy('act', dstt[0:n, 1, 0:n], ps[0:n, 128:128 + n])
                            self.tt('dve', dstt[0:n, 2, 0:n], ps[0:n, 256:256 + n], Z_, ALU.add)
                    return st
                for lev in range(1, nlev + 2):
                    steps.append(mk(lev))
                return steps

            def t6_stages(ci):
                c, n, kind = chunks[ci]
                stages = []
                if kind == 'p':
                    def a1():
                        ps = self.pbank()
                        self.mm(ps[:, 0:128], kT[:, c:c + n], Sbf[:, :])
                        self.stt(Rr[:, :], ps[:, 0:128], nbg[:, ci, h:h + 1], VB[:, ci, :], ALU.mult, ALU.add)

                    def a2():
                        ps2 = self.pbank()
                        self.mm(ps2[:, 0:128], Yf[:, ci, :], Rr[:, :])
                        self.tcopy('act', vn[:, :], ps2[:, 0:128])

                    def a3():
                        ps3 = self.pbank()
                        self.mm(ps3[:, 0:128], Sbf[:, :], qdT[:, c:c + n], start=True, stop=False, count=False)
                        self.mm(ps3[:, 0:128], vn[:, :], AttnT[:, ci, :], start=False, stop=True)
                        ps4 = self.pbank()
                        self.mm(ps4[:, 0:128], Ktl[:, ci, :], vn[:, :])
                        self.stt(self.Sst[:, h, :], self.Sst[:, h, :], self.glc[:, ci:ci + 1], ps4[:, 0:128], ALU.mult, ALU.add)
                        self.tcopy('act', Sbf[:, :], self.Sst[:, h, :])
                        self.tcopy('act', o_raw[:, c:c + n], ps3[:, 0:128])
                    stages += [a1, a2, a3]
                else:
                    def b1():
                        psK = self.pbank()
                        for s in range(16):
                            self.mm(psK[:, s:64:16], Ssb[:, s, :], kT[:, Lp + s:Lp + 64:16], count=(s == 15))
                        self.tcopy('dve', KSTs[:, 0:64], psK[:, 0:64])

                    def b2():
                        psT = self.pbank()
                        self.tr(psT[0:64, 0:128], KSTs[:, 0:64], self.ident(128))
                        self.stt(Rrs[0:64, :], psT[0:64, 0:128], nbg[0:64, ci, h:h + 1], VB[0:64, ci, :], ALU.mult, ALU.add)

                    def b3():
                        ps2 = self.pbank()
                        self.mm(ps2[0:64, 0:128], Yf[0:64, ci, 0:64], Rrs[0:64, :])
                        self.tcopy('act', vns[0:64, :], ps2[0:64, 0:128])

                    def b4():
                        psQ = self.pbank()
                        for s in range(16):
                            self.mm(psQ[:, s:64:16], Ssb[:, s, :], qdT[:, Lp + s:Lp + 64:16], count=(s == 15))
                        self.tcopy('act', otmp[:, 0:64], psQ[:, 0:64])
                        psA = self.pbank()
                        self.mm(psA[:, 0:64], vns[0:64, :], AttnT[0:64, ci, 0:64])
                        self.tt('dve', o_raw[:, c:c + 64], otmp[:, 0:64], psA[:, 0:64], ALU.add)
                    stages += [b1, b2, b3, b4]

                    def mkq(qt):
                        def bq():
                            Ssf = Ssf2[qt % 2]
                            if qt == 0:
                                self.dma('sp', Ssf2[0][:, :, :], d['st_dn'][l][0:4, h].rearrange("s k v -> k s v"))
                            if qt < 3:
                                self.dma('sp', Ssf2[(qt + 1) % 2][:, :, :], d['st_dn'][l][4 * qt + 4:4 * qt + 8, h].rearrange("s k v -> k s v"))
                            for s4 in range(4):
                                s = 4 * qt + s4
                                self.ts('dve', Km[0:64, s % 2, :], Ktl[0:64, ci, :], cst[0:64, C_SEQM + s:C_SEQM + s + 1], ALU.mult)
                                ps = self.pbank()
                                self.mm(ps[:, 0:128], Km[:, s % 2, :], vns[:, :])
                                self.stt(Ssf[:, s4, :], Ssf[:, s4, :], self.gls[:, s:s + 1], ps[:, 0:128], ALU.mult, ALU.add)
                            self.dma('sp', d['ndn_s'][l][4 * qt:4 * qt + 4, h].rearrange("s k v -> k s v"), Ssf[:, :, :], is_out=True)
                        return bq
                    stages += [mkq(qt) for qt in range(4)]
                return stages

            if nch > 8:
                groups = [[8, 0, 1, 2], [3, 4, 5, 6], [7]]
            else:
                groups = [[0, 1, 2, 3], [4, 5, 6, 7]]
            bg = []
            for grp in groups:
                lists = [pre_steps(ci, gi) for gi, ci in enumerate(grp)]
                nst = max(len(x) for x in lists)
                for si in range(nst):
                    for x in lists:
                        if si < len(x):
                            x[si]()
                    for _ in range(2):
                        if bg:
                            bg.pop(0)()
                while bg:
                    bg.pop(0)()
                ssts = []
                psts = []
                for ci in grp:
                    (ssts if chunks[ci][2] == 's' else psts).extend(t6_stages(ci))
                while ssts or psts:
                    if psts:
                        bg.append(psts.pop(0))
                    if ssts:
                        bg.append(ssts.pop(0))
            while bg:
                bg.pop(0)()
            if pi == 1:
                self.dma('sp', d['ndn_p'][l][h], self.Sst[:, h, :], is_out=True)
            (wz,) = self.wget([self.wspec_cols(w_in, 2048 + h * 128, 128)])
            for (c, n, kind) in tiles:
                ps = self.pbank()
                self.mmg(ps[:, 0:n], [(wz[:, kc, :], u[:, kc, c:c + n]) for kc in range(KC)])
                self.act(slf[:, 0:n], ps[:, 0:n], AF.Silu)
                self.act(self.tmp_sq[:, 0, 0:n], o_raw[:, c:c + n], AF.Square)
                prs = self.rstd_psum([self.tmp_sq[:, 0, 0:n]], n, 1.0 / 128, RMS_EPS)
                self.rstd_psum_fin(prs, n, 1.0 / 128, RMS_EPS, stage=0)
                self.rstd_psum_fin(prs, n, 1.0 / 128, RMS_EPS, stage=1)
                self.tt('dve', slf2[:, 0:n], o_raw[:, c:c + n], prs[:, 0:n], ALU.mult)
                self.stt(o_dn[:, h, c:c + n], slf2[:, 0:n], self.vec[:, l, V_NORMG:V_NORMG + 1], slf[:, 0:n], ALU.mult, ALU.mult)
        if pi == 1:
            sto = self.alloc([1536], F32)
            for g in range(3):
                ps = self.pbank()
                for j in range(4):
                    self.tr(ps[0:3, j * 128:(j + 1) * 128], self.tails_dn[:, 4 * g + j, :], self.ident(128))
                self.tcopy('dve', sto[0:3, g * 512:(g + 1) * 512], ps[0:3, :])
            self.dma('sp', d['ndnc_p'][l], sto[0:3, :], is_out=True)


def build_program(debug=None, stop_after=None, names=None):
    b = Builder(debug=debug, stop_after=stop_after)
    b.p.names = names
    nc = b.build()
    return nc, b


def make_in_maps(inp, cores):
    consts = make_consts()
    vecs = make_vecs(inp)
    f = lambda a: np.ascontiguousarray(a, dtype=np.float32)
    maps = []
    for c in cores:
        sl = slice(16 * c, 16 * c + 16)
        maps.append({
            'xp': f(inp['x_prompt'][c]), 'xs': f(inp['x_sample'][sl]),
            'st_pool': f(inp['state_pool'][:, sl]), 'st_dnc': f(inp['state_dn_conv'][:, sl]),
            'st_dn': f(inp['state_dn'][:, sl]), 'st_cm': f(inp['state_cm_conv'][:, sl]),
            'pp': f(inp['p_prompt'][:, c]), 'psm': f(inp['p_sample'][:, sl]),
            'w_up1': f(inp['w_ffn1_up']), 'w_dn1': f(inp['w_ffn1_down']),
            'w_in': f(inp['w_in']), 'pool_w': f(inp['pool_w']),
            'w_br': f(inp['w_branch']), 'w_out': f(inp['w_out']),
            'w_up2': f(inp['w_ffn2_up']), 'w_dn2': f(inp['w_ffn2_down']),
            'w_pg': f(inp['w_ple_gate']), 'w_pp': f(inp['w_ple_proj']),
            'vecs': vecs, 'consts': consts,
        })
    return maps


def kernel(**inp):
    inp = {k: np.asarray(v) for k, v in inp.items()}
    nc, _ = build_program()
    maps = make_in_maps(inp, list(range(NCORE)))
    res = run_bass_kernel_spmd(nc, maps, core_ids=list(range(NCORE)))
    R = res.results
    y_p = np.stack([R[c]['y_p'] for c in range(NCORE)], 0)
    y_s = np.concatenate([R[c]['y_s'] for c in range(NCORE)], 0)

    def pcat(name):
        return np.stack([R[c][name] for c in range(NCORE)], 1)

    def scat(name):
        return np.concatenate([R[c][name] for c in range(NCORE)], 1)

    outs = (y_p, y_s, pcat('npool_p'), pcat('ndnc_p'), pcat('ndn_p'), pcat('ncm_p'),
            scat('npool_s'), scat('ndnc_s'), scat('ndn_s'), scat('ncm_s'))
    return tuple(np.ascontiguousarray(o, dtype=np.float32) for o in outs)
```

```python
import numpy as np
import concourse.bass as bass
import concourse.mybir as mybir
from concourse.bass_utils import run_bass_kernel_spmd

F32 = mybir.dt.float32
BF16 = mybir.dt.bfloat16
ALU = mybir.AluOpType
AF = mybir.ActivationFunctionType

D = 1024
KC = 8
DFF = 2816
NPR = 2048
NSM = 64
NT = NPR + NSM
DEPTH = 2
NCORE = 8
WIN = 6664
RMS_EPS = 1e-6
LN_EPS = 1e-5
NEG = -1.0e5

C_ID = 0
C_ONES = 128
C_M1 = 256
C_NEGI = 384
C_STRICT = 512
C_M1S = 640
C_ALLS = 768
C_NEGIS = 896
C_STRICTS = 1024
C_SEQM = 1152
C_SEL = 1168
C_DSELB = C_SEL + 1024
C_DSELA = C_DSELB + 8
C_INVCNT = C_DSELA + 8
C_INVW = C_INVCNT + 64
NCON = C_INVW + 4

V_GFFN1 = 0
V_GMIX = 8
V_GFFN2 = 16
V_GPLE = 24
V_PSCALE = 32
V_CMB = 36
V_LNG = 40
V_LNB = 44
V_NORMG = 48
V_DNW = 49
V_CMW = 97
V_ALOG = 221
V_DTB = 222
V_GFIN = 223
NVEC = 231


def make_consts():
    c = np.zeros((128, NCON), np.float32)
    i = np.arange(128)
    c[:, C_ID:C_ID + 128] = np.eye(128)
    c[:, C_ONES:C_ONES + 128] = 1.0
    c[:, C_M1:C_M1 + 128] = (i[:, None] <= i[None, :])
    c[:, C_NEGI:C_NEGI + 128] = np.where(i[None, :] >= i[:, None], 0.0, NEG)
    c[:, C_STRICT:C_STRICT + 128] = (i[None, :] > i[:, None])
    j = np.arange(64)
    same = (j[:, None] % 16) == (j[None, :] % 16)
    m1s = np.zeros((128, 128), np.float32)
    m1s[:64, :64] = same & (j[:, None] <= j[None, :])
    c[:, C_M1S:C_M1S + 128] = m1s
    alls = np.zeros((128, 128), np.float32)
    alls[:64, :64] = same
    c[:, C_ALLS:C_ALLS + 128] = alls
    negs = np.full((128, 128), NEG, np.float32)
    negs[:64, :64] = np.where(same & (j[None, :] >= j[:, None]), 0.0, NEG)
    c[:, C_NEGIS:C_NEGIS + 128] = negs
    st = np.zeros((128, 128), np.float32)
    st[:64, :64] = same & (j[None, :] > j[:, None])
    c[:, C_STRICTS:C_STRICTS + 128] = st
    sq = np.zeros((128, 16), np.float32)
    sq[:64] = (j[:, None] % 16) == np.arange(16)[None, :]
    c[:, C_SEQM:C_SEQM + 16] = sq
    for r in range(8):
        c[r, C_SEL + r * 128:C_SEL + (r + 1) * 128] = 1.0
    for r in range(4, 8):
        c[r, C_DSELB + r] = 1.0
    for r in range(4):
        c[r, C_DSELA + r] = 1.0
    for g, w in enumerate((2, 4, 8, 16)):
        for t in range(16):
            c[:, C_INVCNT + g * 16 + t] = 1.0 / min(t + 1, w)
        c[:, C_INVW + g] = 1.0 / w
    return c


def make_vecs(inp):
    v = np.zeros((DEPTH, 128, NVEC), np.float32)
    for l in range(DEPTH):
        def pk(a, n):
            return np.ascontiguousarray(a.reshape(n, 128).T)
        v[l, :, V_GFFN1:V_GFFN1 + 8] = pk(inp['g_ffn1'][l], 8)
        v[l, :, V_GMIX:V_GMIX + 8] = pk(inp['g_mix'][l], 8)
        v[l, :, V_GFFN2:V_GFFN2 + 8] = pk(inp['g_ffn2'][l], 8)
        v[l, :, V_GPLE:V_GPLE + 8] = pk(inp['g_ple'][l], 8)
        v[l, :, V_PSCALE:V_PSCALE + 4] = pk(inp['pool_scale'][l], 4)
        v[l, :, V_CMB:V_CMB + 4] = pk(inp['cm_dw_b'][l], 4)
        v[l, :, V_LNG:V_LNG + 4] = pk(inp['cm_ln_g'][l], 4)
        v[l, :, V_LNB:V_LNB + 4] = pk(inp['cm_ln_b'][l], 4)
        v[l, :, V_NORMG] = inp['dn_norm_g'][l]
        v[l, :, V_DNW:V_DNW + 48] = inp['dn_conv_w'][l].reshape(4, 12, 128).transpose(2, 1, 0).reshape(128, 48)
        v[l, :, V_CMW:V_CMW + 124] = inp['cm_dw_w'][l].reshape(31, 4, 128).transpose(2, 1, 0).reshape(128, 124)
        v[l, 0:4, V_ALOG] = inp['dn_A_log'][l]
        v[l, 0:4, V_DTB] = inp['dn_dt_bias'][l]
        v[l, :, V_GFIN:V_GFIN + 8] = pk(inp['g_final'], 8)
    return v


class View:
    __slots__ = ('ap', 'space', 'runs')

    def __init__(self, ap, space, runs):
        self.ap = ap
        self.space = space
        self.runs = runs


class Rec:
    __slots__ = ('lo', 'hi', 'w', 'r')

    def __init__(self, lo, hi, w, r):
        self.lo = lo
        self.hi = hi
        self.w = w
        self.r = r


class Space:
    def __init__(self):
        self.recs = []

    def carve(self, lo, hi):
        out = []
        new = []
        cur = lo
        for r in self.recs:
            if r.hi <= lo or r.lo >= hi:
                new.append(r)
                continue
            if r.lo < lo:
                new.append(Rec(r.lo, lo, r.w, dict(r.r)))
                r.lo = lo
            right = None
            if r.hi > hi:
                right = Rec(hi, r.hi, r.w, dict(r.r))
                r.hi = hi
            if r.lo > cur:
                g = Rec(cur, r.lo, None, {})
                new.append(g)
                out.append(g)
            new.append(r)
            out.append(r)
            cur = r.hi
            if right is not None:
                new.append(right)
        if cur < hi:
            g = Rec(cur, hi, None, {})
            new.append(g)
            out.append(g)
        new.sort(key=lambda x: x.lo)
        self.recs = new
        return out

    def write_commit(self, lo, hi, tok):
        self.recs = [r for r in self.recs if r.hi <= lo or r.lo >= hi]
        self.recs.append(Rec(lo, hi, tok, {}))
        self.recs.sort(key=lambda x: x.lo)


class Prog:
    ENG = ('pe', 'act', 'dve', 'pool', 'sp')

    def __init__(self, nc, n_dma_sems=24):
        self.nc = nc
        self.semobj = {}
        self.own = {}
        for e in self.ENG:
            nm = 'sem_' + e
            self.semobj[nm] = nc.alloc_semaphore(nm)
            self.own[e] = nm
        self.cnt = {e: 0 for e in self.ENG}
        self.stream = {e: [] for e in self.ENG}
        self.waited = {e: {} for e in self.ENG}
        self.spaces = {'sb': Space(), 'ps': Space()}
        self.dma_pool = []
        self.dma_cnt = {}
        for i in range(n_dma_sems):
            nm = 'sem_dma%d' % i
            self.semobj[nm] = nc.alloc_semaphore(nm)
            self.dma_pool.append(nm)
            self.dma_cnt[nm] = 0
        self.dma_rr = 0
        self.dma_pool_sw = []
        for i in range(6):
            nm = 'sem_swdma%d' % i
            self.semobj[nm] = nc.alloc_semaphore(nm)
            self.dma_pool_sw.append(nm)
            self.dma_cnt[nm] = 0
        self.dma_rr_sw = 0
        self.out_tokens = []
        self.n_ops = 0
        self.n_waits = 0
        self.dry = False
        self.label = ''
        self.names = None

    def new_dma_sem(self, name):
        self.semobj[name] = self.nc.alloc_semaphore(name)
        self.dma_cnt[name] = 0
        return name

    def op(self, e, fn, reads=(), writes=(), count=True, dma=None, is_out=False):
        if self.dry:
            return None
        deps = {}
        own = self.own[e]

        def add(tok, raw):
            if tok is None:
                return
            s, v = tok
            if s == own and e == 'pe':
                return
            if deps.get(s, 0) < v:
                deps[s] = v

        rrecs = []
        ps_reads = [vw for vw in reads if vw is not None and vw.space == 'ps']
        reads = [vw for vw in reads if vw is not None and vw.space != 'ps']
        writes = list(writes) + ps_reads
        writes = [vw if vw.space != 'ps' else View(vw.ap, 'ps', [(lo // 2048 * 2048, (hi - 1) // 2048 * 2048 + 2048) for lo, hi in vw.runs]) for vw in writes]
        for vw in reads:
            sp = self.spaces[vw.space]
            for lo, hi in vw.runs:
                for r in sp.carve(lo, hi):
                    add(r.w, True)
                    rrecs.append(r)
        n_real_w = len(writes) - len(ps_reads)
        for wi, vw in enumerate(writes):
            sp = self.spaces[vw.space]
            for lo, hi in vw.runs:
                for r in sp.carve(lo, hi):
                    add(r.w, wi >= n_real_w)
                    for s, v in r.r.items():
                        add((s, v), False)
        dsem = None
        if dma is not None:
            if dma == 'auto' and e == 'pool':
                dsem = self.dma_pool_sw[self.dma_rr_sw % len(self.dma_pool_sw)]
                self.dma_rr_sw += 1
            elif dma == 'auto':
                dsem = self.dma_pool[self.dma_rr % len(self.dma_pool)]
                self.dma_rr += 1
            else:
                dsem = dma
            if self.dma_cnt[dsem] > 0:
                add((dsem, self.dma_cnt[dsem]), False)
            self.dma_cnt[dsem] += 16
            tok = (dsem, self.dma_cnt[dsem])
        elif count:
            self.cnt[e] += 1
            tok = (own, self.cnt[e])
        else:
            tok = (own, self.cnt[e] + 1)
        waits = []
        wd = self.waited[e]
        for s, v in deps.items():
            if wd.get(s, 0) >= v:
                continue
            wd[s] = v
            waits.append((self.semobj[s], v))
        for vw in reads:
            sp = self.spaces[vw.space]
            for lo, hi in vw.runs:
                for r in sp.carve(lo, hi):
                    if r.r.get(tok[0], 0) < tok[1]:
                        r.r[tok[0]] = tok[1]
        for vw in writes:
            sp = self.spaces[vw.space]
            for lo, hi in vw.runs:
                sp.write_commit(lo, hi, tok)
        if is_out:
            self.out_tokens.append(tok)
        self.n_ops += 1
        self.n_waits += len(waits)
        semh = self.semobj[tok[0]]
        is_dma = dma is not None

        label = self.label
        names = self.names

        def emit(engobj):
            for s, v in waits:
                engobj.wait_ge(s, v)
            ins = fn(engobj)
            if names is not None:
                names.append((e, str(ins.ins.name), label))
            if is_dma:
                ins.then_inc(semh, 16)
            elif count:
                ins.then_inc(semh, 1)

        self.stream[e].append(emit)
        return tok

    def finish(self):
        fin = {}
        for s, v in self.out_tokens:
            if fin.get(s, 0) < v:
                fin[s] = v
        finw = [(self.semobj[s], v) for s, v in fin.items()]

        def fin_emit(engobj):
            for s, v in finw:
                engobj.wait_ge(s, v)

        self.stream['sp'].append(fin_emit)
        nc = self.nc
        st = self.stream
        with nc.Block() as block:
            @block.tensor
            def _(eng):
                for f in st['pe']:
                    f(eng)

            @block.scalar
            def _(eng):
                for f in st['act']:
                    f(eng)

            @block.vector
            def _(eng):
                for f in st['dve']:
                    f(eng)

            @block.gpsimd
            def _(eng):
                for f in st['pool']:
                    f(eng)

            @block.sync
            def _(eng):
                for f in st['sp']:
                    f(eng)


def _norm_idx(ix, n):
    if isinstance(ix, slice):
        a, b, s = ix.indices(n)
        return a, b, s
    return ix, ix + 1, 1


class T:
    def __init__(self, ap, space, off, esz, dims):
        self.ap = ap
        self.space = space
        self.off = off
        self.esz = esz
        self.dims = tuple(dims)

    def __getitem__(self, idx):
        if not isinstance(idx, tuple):
            idx = (idx,)
        idx = list(idx) + [slice(None)] * (1 + len(self.dims) - len(idx))
        ap = self.ap[tuple(idx)]
        fr = [_norm_idx(ix, n) for ix, n in zip(idx[1:], self.dims)]
        strides = []
        acc = 1
        for n in reversed(self.dims):
            strides.append(acc)
            acc *= n
        strides = strides[::-1]
        runs = []

        def rec(d, base):
            a, b, s = fr[d]
            if d == len(fr) - 1:
                runs.append((base + a, base + a + (b - a - 1) // s * s + 1 if b > a else base + a))
                return
            for i in range(a, b, s):
                rec(d + 1, base + i * strides[d])

        rec(0, 0)
        runs.sort()
        merged = []
        for lo, hi in runs:
            if merged and lo <= merged[-1][1]:
                merged[-1][1] = max(merged[-1][1], hi)
            else:
                merged.append([lo, hi])
        bruns = [(self.off + lo * self.esz, self.off + hi * self.esz) for lo, hi in merged]
        return View(ap, self.space, bruns)


class Builder:
    NSLOT = 6
    SLOT_E = 2048
    ARENA_F32 = 52000

    def __init__(self, debug=None, stop_after=None):
        self.debug = debug or {}
        self.stop_after = stop_after
        nc = bass.Bass("TRN2", target_bir_lowering=False)
        self.nc = nc
        self.p = Prog(nc)
        d = {}

        def din(name, shape):
            d[name] = nc.dram_tensor(name, list(shape), F32, kind="ExternalInput").ap()

        def dout(name, shape):
            d[name] = nc.dram_tensor(name, list(shape), F32, kind="ExternalOutput").ap()

        din('xp', [NPR, D]); din('xs', [16, 4, D])
        din('st_pool', [2, 16, 15, 512]); din('st_dnc', [2, 16, 3, 1536])
        din('st_dn', [2, 16, 4, 128, 128]); din('st_cm', [2, 16, 30, 512])
        din('pp', [2, NPR, 256]); din('psm', [2, 16, 4, 256])
        din('w_up1', [2, D, 2 * DFF]); din('w_dn1', [2, DFF, D])
        din('w_in', [2, D, WIN]); din('pool_w', [2, 4, 128, 128])
        din('w_br', [2, 3, 512, D]); din('w_out', [2, D, D])
        din('w_up2', [2, D, 2 * DFF]); din('w_dn2', [2, DFF, D])
        din('w_pg', [2, D, D]); din('w_pp', [2, 256, D])
        din('vecs', [2, 128, NVEC]); din('consts', [128, NCON])
        dout('y_p', [NPR, D]); dout('y_s', [16, 4, D])
        dout('npool_p', [2, 15, 512]); dout('ndnc_p', [2, 3, 1536])
        dout('ndn_p', [2, 4, 128, 128]); dout('ncm_p', [2, 30, 512])
        dout('npool_s', [2, 16, 15, 512]); dout('ndnc_s', [2, 16, 3, 1536])
        dout('ndn_s', [2, 16, 4, 128, 128]); dout('ncm_s', [2, 16, 30, 512])
        self.dbg_out = {}
        for name, shape in self.debug.items():
            self.dbg_out[name] = nc.dram_tensor('dbg_' + name, list(shape), F32, kind="ExternalOutput").ap()
        self.d = d
        self.arena = nc.alloc_sbuf_tensor("arena", [128, self.ARENA_F32], F32)
        self.psb = [nc.alloc_psum_tensor("psb%d" % i, [128, 512], F32) for i in range(8)]
        self.ring_sems = [self.p.new_dma_sem('sem_ring%d' % i) for i in range(self.NSLOT)]
        self.wplan = []

    def alloc(self, dims, dtype=F32):
        esz = 4 if dtype == F32 else 2
        n = int(np.prod(dims))
        nbytes = (n * esz + 31) // 32 * 32
        off = self.top
        self.top += nbytes
        assert self.top <= self.ARENA_F32 * 4, ("arena overflow", self.top)
        self.peak = max(self.peak, self.top)
        ap = self.arena[:, off // 4: (off + nbytes) // 4]
        if dtype == BF16:
            ap = ap.bitcast(BF16)
        ap = ap[:, 0:n]
        if len(dims) == 2:
            ap = ap.rearrange("p (a b) -> p a b", a=dims[0])
        elif len(dims) == 3:
            ap = ap.rearrange("p (a b c) -> p a b c", a=dims[0], b=dims[1])
        return T(ap, 'sb', off, esz, dims)

    def pbank(self):
        b = self.ps_next % 8
        self.ps_next += 1
        return T(self.psb[b], 'ps', b * 2048, 4, (512,))

    def pbank_bf(self):
        b = self.ps_next % 8
        self.ps_next += 1
        return T(self.psb[b][:, :].bitcast(BF16), 'ps', b * 2048, 2, (1024,))

    def mm(self, out, lhsT, rhs, start=True, stop=True, count=True):
        self.p.op('pe', lambda e, o=out.ap, l=lhsT.ap, r=rhs.ap: e.matmul(o, lhsT=l, rhs=r, start=start, stop=stop),
                  reads=[lhsT, rhs], writes=[out], count=count)

    def mmg(self, out, pairs):
        n = len(pairs)
        for i, (l, r) in enumerate(pairs):
            self.mm(out, l, r, start=(i == 0), stop=(i == n - 1), count=(i == n - 1))

    def tr(self, out, in_, ident):
        self.p.op('pe', lambda e, o=out.ap, i=in_.ap, d=ident.ap: e.transpose(out=o, in_=i, identity=d),
                  reads=[in_, ident], writes=[out])

    def act(self, out, in_, func, bias=None, scale=None, eng='act'):
        kw = {}
        rd = [in_]
        if bias is not None:
            if isinstance(bias, View):
                kw['bias'] = bias.ap
                rd.append(bias)
            else:
                kw['bias'] = float(bias)
        if scale is not None:
            if isinstance(scale, View):
                kw['scale'] = scale.ap
                rd.append(scale)
            else:
                kw['scale'] = float(scale)
        self.p.op('act', lambda e, o=out.ap, i=in_.ap: e.activation(out=o, in_=i, func=func, **kw),
                  reads=rd, writes=[out])

    def tcopy(self, eng, out, in_):
        if eng == 'act':
            self.p.op('act', lambda e, o=out.ap, i=in_.ap: e.copy(out=o, in_=i), reads=[in_], writes=[out])
        else:
            self.p.op(eng, lambda e, o=out.ap, i=in_.ap: e.tensor_copy(out=o, in_=i), reads=[in_], writes=[out])

    def tt(self, eng, out, in0, in1, op):
        self.p.op(eng, lambda e, o=out.ap, a=in0.ap, b=in1.ap: e.tensor_tensor(out=o, in0=a, in1=b, op=op),
                  reads=[in0, in1], writes=[out])

    def ts(self, eng, out, in0, s1, op0, s2=None, op1=None):
        rd = [in0]
        a1 = s1
        if isinstance(s1, View):
            rd.append(s1)
            a1 = s1.ap
        a2 = s2
        if isinstance(s2, View):
            rd.append(s2)
            a2 = s2.ap
        kw = {}
        if op1 is not None:
            kw['op1'] = op1
        self.p.op(eng, lambda e, o=out.ap, i=in0.ap: e.tensor_scalar(out=o, in0=i, scalar1=a1, scalar2=a2, op0=op0, **kw),
                  reads=rd, writes=[out])

    def stt(self, out, in0, scalar, in1, op0, op1):
        rd = [in0, in1]
        sc = scalar
        if isinstance(scalar, View):
            rd.append(scalar)
            sc = scalar.ap
        self.p.op('dve', lambda e, o=out.ap, a=in0.ap, b=in1.ap: e.scalar_tensor_tensor(out=o, in0=a, scalar=sc, in1=b, op0=op0, op1=op1),
                  reads=rd, writes=[out])

    def memset(self, eng, out, val):
        self.p.op(eng, lambda e, o=out.ap: e.memset(o, val), writes=[out])

    def dma(self, eng, out, in_, is_out=False, sem='auto'):
        rd = [in_] if isinstance(in_, View) else []
        wr = [out] if isinstance(out, View) else []
        oa = out.ap if isinstance(out, View) else out
        ia = in_.ap if isinstance(in_, View) else in_
        self.p.op(eng, lambda e: e.dma_start(out=oa, in_=ia), reads=rd, writes=wr, dma=sem, is_out=is_out)

    def wget(self, specs):
        if self.p.dry:
            i0 = len(self.wplan)
            self.wplan.extend(specs)
        else:
            i0 = self.wnext
            self.wnext += len(specs)
            lim = min(len(self.wplan), i0 + self.NSLOT)
            while self.wissued < lim:
                j = self.wissued
                ap, A, B = self.wplan[j]
                sl = self.slots[j % self.NSLOT]
                dst = sl[:, 0:A * B]
                dst = View(dst.ap.rearrange("p (a b) -> p a b", a=A), dst.space, dst.runs)
                self.dma('pool', dst, ap, sem=self.ring_sems[j % self.NSLOT])
                self.wissued += 1
        out = []
        for k, (ap, A, B) in enumerate(specs):
            sl = self.slots[(i0 + k) % self.NSLOT]
            out.append(T(sl.ap[:, 0:A * B].rearrange("p (a b) -> p a b", a=A), 'sb', sl.off, 2, (A, B)))
        return out

    def wspec_cols(self, w2d, c0, n, kcn=KC):
        return (w2d[:, c0:c0 + n].rearrange("(kc p) n -> p kc n", p=128), kcn, n)

    def wspec_rows(self, w2d, r0, nch):
        ncol = w2d.shape[1]
        return (w2d[r0 * 128:(r0 + nch) * 128, :].rearrange("(j p) n -> p j n", p=128), nch, ncol)

    @staticmethod
    def ttiles(c0, n, step=512):
        out = []
        c = c0
        while c < c0 + n:
            w = min(step, c0 + n - c)
            out.append((c, w))
            c += w
        return out

    def build(self):
        self.p.dry = True
        self.emit_all()
        self.p.dry = False
        self.emit_all()
        self.p.finish()
        return self.nc

    def emit_all(self):
        self.top = 0
        self.peak = 0
        self.ps_next = 0
        self.wnext = 0
        self.wissued = 0
        self.dbg_done = set()
        d = self.d
        self.xT = self.alloc([KC, NT], F32)
        self.cst = self.alloc([NCON], F32)
        self.vec = self.alloc([DEPTH, NVEC], F32)
        self.idb = self.alloc([128], BF16)
        self.oneb = self.alloc([128], BF16)
        self.slots = [self.alloc([self.SLOT_E], BF16) for _ in range(self.NSLOT)]
        self.tails_pool = self.alloc([4, 15], F32)
        self.tails_dn = self.alloc([12, 3], F32)
        self.tails_cm = self.alloc([4, 30], BF16)
        self.Sst = self.alloc([4, 128], F32)
        self.glc = self.alloc([16], F32)
        self.gls = self.alloc([16], F32)
        self.phase0 = self.top
        self.stage = [self.alloc([D], F32) for _ in range(2)]
        self.dma('sp', self.cst[:, :], d['consts'])
        self.dma('sp', self.vec[:, :, :], d['vecs'].rearrange("l p n -> p l n"))
        self.tcopy('dve', self.idb[:, :], self.cst[:, C_ID:C_ID + 128])
        self.tcopy('dve', self.oneb[:, :], self.cst[:, C_ONES:C_ONES + 128])
        self.ident = lambda n=128: self.cst[0:n, C_ID:C_ID + n]
        self.tap('cst', self.cst)
        if self.stop_after == 'consts':
            return
        self.p.label = 'load'
        self.load_x()
        if self.stop_after == 'loadx':
            self.tap('x0', self.xT)
            return
        self.tap('x0', self.xT)
        if self.stop_after == 'load':
            return self.final()
        for l in range(DEPTH):
            self.p.label = 'ffn1_%d' % l
            self.ffn(l, d['w_up1'][l], d['w_dn1'][l], V_GFFN1)
            self.tap('x1_%d' % l, self.xT)
            if self.stop_after == 'ffn1_%d' % l:
                return self.final()
            self.mixer(l)
            self.tap('x2_%d' % l, self.xT)
            if self.stop_after == 'mix_%d' % l:
                return self.final()
            self.p.label = 'ffn2_%d' % l
            self.ffn(l, d['w_up2'][l], d['w_dn2'][l], V_GFFN2)
            self.tap('x3_%d' % l, self.xT)
            self.p.label = 'ple_%d' % l
            self.ple(l)
            self.tap('x4_%d' % l, self.xT)
            if self.stop_after == 'layer_%d' % l:
                return self.final()
        self.final()

    def tap(self, name, tile, view=None, eng='sp'):
        if name not in self.dbg_out or name in self.dbg_done:
            return
        self.dbg_done.add(name)
        v = view if view is not None else tile[(slice(None),) * (1 + len(tile.dims))]
        self.dma(eng, self.dbg_out[name], v, is_out=True)

    def load_x(self):
        d = self.d
        ev = 0
        for i in range(17):
            st = self.stage[i % 2]
            if i < 16:
                n = 128
                self.dma('sp', st[:, :], d['xp'][i * 128:(i + 1) * 128, :])
                c0 = i * 128
            else:
                n = 64
                for t in range(4):
                    self.dma('sp', st[16 * t:16 * t + 16, :], d['xs'][:, t, :])
                c0 = NPR
            for g in range(2):
                ps = self.pbank()
                for j in range(4):
                    kc = 4 * g + j
                    self.tr(ps[:, j * 128:j * 128 + n], st[0:n, kc * 128:(kc + 1) * 128], self.ident(n))
                for j in range(4):
                    kc = 4 * g + j
                    self.tcopy('act' if ev % 2 else 'dve', self.xT[:, kc, c0:c0 + n], ps[:, j * 128:j * 128 + n])
                    ev += 1

    def rstd_tile(self, srcs, n, scale, eps, bias2=0.0):
        sq = self.tmp_sq
        for k, s in enumerate(srcs):
            self.act(sq[:, k, 0:n], s, AF.Square)
        ps = self.pbank()
        self.mmg(ps[:, 0:n], [(self.oneb[:, :], sq[:, k, 0:n]) for k in range(len(srcs))])
        self.act(self.tmp_ln[:, 0:n], ps[:, 0:n], AF.Ln, bias=self.epsv(eps), scale=scale)
        self.act(self.tmp_rs[:, 0:n], self.tmp_ln[:, 0:n], AF.Exp, scale=-0.5, bias=self.epsv(bias2) if bias2 != 0.0 else None)
        return self.tmp_rs[:, 0:n]

    def rstd_psum(self, sq_views, n, scale, eps, bias2=0.0):
        ps = self.pbank()
        self.mmg(ps[:, 0:n], [(self.oneb[:, :], v) for v in sq_views])
        return ps

    def rstd_psum_fin(self, ps, n, scale, eps, bias2=0.0, stage=0):
        if stage == 0:
            self.act(ps[:, 0:n], ps[:, 0:n], AF.Ln, bias=self.epsv(eps), scale=scale)
        else:
            self.act(ps[:, 0:n], ps[:, 0:n], AF.Exp, scale=-0.5, bias=self.epsv(bias2) if bias2 != 0.0 else None)

    def epsv(self, val, p=128):
        key = float(val)
        if key not in self.eps_cols:
            idx = len(self.eps_cols)
            assert idx < 8
            self.eps_cols[key] = idx
            self.memset('dve', self.epst[:, idx:idx + 1], key)
        i = self.eps_cols[key]
        return self.epst[0:p, i:i + 1]

    def norm_tmps(self, nsq=KC, small=False):
        self.tmp_sq = self.alloc([nsq, 512], BF16)
        if not small:
            self.tmp_ln = self.alloc([512], F32)
            self.tmp_rs = self.alloc([512], F32)
        self.epst = self.alloc([8], F32)
        self.eps_cols = {}

    def rmsnorm_full(self, l, vcol, u, c0=0, ncols=NT, ucol0=0):
        for (c, n) in self.ttiles(c0, ncols):
            rs = self.rstd_tile([self.xT[:, kc, c:c + n] for kc in range(KC)], n, 1.0 / D, RMS_EPS)
            for kc in range(KC):
                self.stt(u[:, kc, ucol0 + c - c0:ucol0 + c - c0 + n], self.xT[:, kc, c:c + n],
                         self.vec[:, l, vcol + kc:vcol + kc + 1], rs, ALU.mult, ALU.mult)

    def final(self):
        d = self.d
        self.p.label = 'final'
        self.top = self.phase0
        self.stage = [self.alloc([D], F32) for _ in range(2)]
        self.norm_tmps()
        yt = self.alloc([KC, 128], F32)
        ev = 0
        for i in range(17):
            n = 128 if i < 16 else 64
            c0 = i * 128
            rs = self.rstd_tile([self.xT[:, kc, c0:c0 + n] for kc in range(KC)], n, 1.0 / D, RMS_EPS)
            for kc in range(KC):
                self.stt(yt[:, kc, 0:n], self.xT[:, kc, c0:c0 + n], self.vec[:, 0, V_GFIN + kc:V_GFIN + kc + 1], rs,
                         ALU.mult, ALU.mult)
            st = self.stage[i % 2]
            for g in range(2):
                ps = self.pbank()
                for j in range(4):
                    kc = 4 * g + j
                    self.tr(ps[0:n, j * 128:(j + 1) * 128], yt[:, kc, 0:n], self.ident(128))
                self.tcopy('act' if ev % 2 else 'dve', st[0:n, g * 512:(g + 1) * 512], ps[0:n, :])
                ev += 1
            if i < 16:
                self.dma('sp', d['y_p'][i * 128:(i + 1) * 128, :], st[:, :], is_out=True)
            else:
                for t in range(4):
                    self.dma('sp', d['y_s'][:, t, :], st[16 * t:16 * t + 16, :], is_out=True)

    def ffn(self, l, w_up, w_dn, vcol):
        self.top = self.phase0
        self.norm_tmps()
        u = self.alloc([KC, NT], BF16)
        h = self.alloc([6, NT], BF16)
        sa = [self.alloc([512], BF16) for _ in range(2)]
        self.rmsnorm_full(l, vcol, u)
        tts = self.ttiles(0, NT)
        groups = [(0, 6), (6, 6), (12, 6), (18, 4)]
        ev = 0
        for (j0, nj) in groups:
            for q in range(j0 // 2, (j0 + nj) // 2):
                wa, wb = self.wget([self.wspec_cols(w_up, 256 * q, 256), self.wspec_cols(w_up, DFF + 256 * q, 256)])
                for jj in range(2):
                    jl = 2 * q + jj - j0
                    for (c, n) in tts:
                        pa = self.pbank()
                        pb = self.pbank()
                        self.mmg(pa[:, 0:n], [(wa[:, kc, jj * 128:(jj + 1) * 128], u[:, kc, c:c + n]) for kc in range(KC)])
                        self.mmg(pb[:, 0:n], [(wb[:, kc, jj * 128:(jj + 1) * 128], u[:, kc, c:c + n]) for kc in range(KC)])
                        s = sa[ev % 2]
                        ev += 1
                        self.act(s[:, 0:n], pa[:, 0:n], AF.Silu)
                        self.tt('dve', h[:, jl, c:c + n], pb[:, 0:n], s[:, 0:n], ALU.mult)
            wd = self.wget([self.wspec_rows(w_dn, j0 + 2 * i, 2) for i in range(nj // 2)])
            for (c, n) in tts:
                for m in range(KC):
                    py = self.pbank()
                    self.mmg(py[:, 0:n], [(wd[j // 2][:, j % 2, m * 128:(m + 1) * 128], h[:, j, c:c + n]) for j in range(nj)])
                    self.stt(self.xT[:, m, c:c + n], py[:, 0:n], 0.5, self.xT[:, m, c:c + n], ALU.mult, ALU.add)

    def ple(self, l):
        d = self.d
        self.top = self.phase0
        self.norm_tmps()
        u = self.alloc([KC, NT], BF16)
        pT = self.alloc([2, NT], BF16)
        sg = [self.alloc([512], F32) for _ in range(2)]
        pr = [self.alloc([512], F32) for _ in range(2)]
        pst = [self.alloc([256], F32) for _ in range(2)]
        for i in range(17):
            st = pst[i % 2]
            if i < 16:
                n = 128
                self.dma('sp', st[:, :], d['pp'][l, i * 128:(i + 1) * 128, :])
            else:
                n = 64
                for t in range(4):
                    self.dma('sp', st[16 * t:16 * t + 16, :], d['psm'][l, :, t, :])
            ps = self.pbank()
            for j in range(2):
                self.tr(ps[:, j * 128:j * 128 + n], st[0:n, j * 128:(j + 1) * 128], self.ident(n))
            for j in range(2):
                self.tcopy('act' if j else 'dve', pT[:, j, i * 128:i * 128 + n], ps[:, j * 128:j * 128 + n])
        self.rmsnorm_full(l, V_GPLE, u)
        tts = self.ttiles(0, NT)
        ev = 0
        for q in range(4):
            wg, wp = self.wget([self.wspec_cols(d['w_pg'][l], 256 * q, 256),
                                self.wspec_cols(d['w_pp'][l], 256 * q, 256, kcn=2)])
            for jj in range(2):
                m = 2 * q + jj
                for (c, n) in tts:
                    pg = self.pbank()
                    pe = self.pbank()
                    self.mmg(pg[:, 0:n], [(wg[:, kc, jj * 128:(jj + 1) * 128], u[:, kc, c:c + n]) for kc in range(KC)])
                    self.mmg(pe[:, 0:n], [(wp[:, kc, jj * 128:(jj + 1) * 128], pT[:, kc, c:c + n]) for kc in range(2)])
                    s = sg[ev % 2]
                    r = pr[ev % 2]
                    ev += 1
                    self.act(s[:, 0:n], pg[:, 0:n], AF.Sigmoid)
                    self.tt('dve', r[:, 0:n], pe[:, 0:n], s[:, 0:n], ALU.mult)
                    self.tt('pool', self.xT[:, m, c:c + n], self.xT[:, m, c:c + n], r[:, 0:n], ALU.add)

    def mixer(self, l):
        for pi in range(2):
            self.mixer_pass(l, pi)

    def mixer_pass(self, l, pi):
        d = self.d
        Lp = 1024
        has_s = (pi == 1)
        W = Lp + (NSM if has_s else 0)
        xc0 = pi * Lp
        w_in = d['w_in'][l]
        self.top = self.phase0
        u = self.alloc([KC, W], BF16)
        o_dn = self.alloc([4, W], BF16)
        mark0 = self.top
        self.norm_tmps()
        self.p.label = 'mixnorm_%d_%d' % (l, pi)
        self.rmsnorm_full(l, V_GMIX, u, c0=xc0, ncols=W, ucol0=0)
        tiles = [(0, 512, 'p'), (512, 512, 'p')] + ([(Lp, NSM, 's')] if has_s else [])
        if has_s:
            self.dma('act', d['npool_s'][l][:, 0:11, :], d['st_pool'][l][:, 4:15, :], is_out=True)
            self.dma('act', d['ncm_s'][l][:, 0:26, :], d['st_cm'][l][:, 4:30, :], is_out=True)
        self.top = mark0
        self.norm_tmps(nsq=1, small=True)
        self.p.label = 'dn_%d_%d' % (l, pi)
        self.dn_branch(l, pi, u, o_dn, tiles, Lp, has_s, W)
        self.tap('odn_%d_%d' % (l, pi), o_dn, eng='pool')
        self.top = mark0
        o_pool = self.alloc([4, W], BF16)
        o_cm = self.alloc([4, W], BF16)
        mark1 = self.top
        self.p.label = 'pool_%d_%d' % (l, pi)
        self.pool_branch(l, pi, u, o_pool, tiles, Lp, has_s, W)
        self.tap('opool_%d_%d' % (l, pi), o_pool, eng='pool')
        self.top = mark1
        self.norm_tmps(nsq=4)
        self.p.label = 'cm_%d_%d' % (l, pi)
        self.cm_branch(l, pi, u, o_cm, tiles, Lp, has_s, W)
        self.tap('ocm_%d_%d' % (l, pi), o_cm, eng='pool')
        self.top = mark1
        self.p.label = 'merge_%d_%d' % (l, pi)
        mg = self.alloc([KC, W], BF16)
        sg = [self.alloc([512], F32) for _ in range(2)]
        pr = [self.alloc([512], F32) for _ in range(2)]
        acc = self.alloc([len(tiles), 512], F32)
        obr = [o_pool, o_dn, o_cm]
        ev = 0
        for m in range(KC):
            for nb in range(3):
                wg_, wb_ = self.wget([self.wspec_cols(w_in, 3592 + nb * 1024 + 128 * m, 128),
                                      self.wspec_cols(d['w_br'][l][nb], 128 * m, 128, kcn=4)])
                for ti, (c, n, kind) in enumerate(tiles):
                    pg = self.pbank()
                    pb = self.pbank()
                    self.mmg(pg[:, 0:n], [(wg_[:, kc, :], u[:, kc, c:c + n]) for kc in range(KC)])
                    self.mmg(pb[:, 0:n], [(wb_[:, kc, :], obr[nb][:, kc, c:c + n]) for kc in range(4)])
                    sgt = sg[ev % 2]
                    prt = pr[ev % 2]
                    ev += 1
                    self.act(sgt[:, 0:n], pg[:, 0:n], AF.Sigmoid)
                    if nb == 0:
                        self.tt('dve', acc[:, ti, 0:n], pb[:, 0:n], sgt[:, 0:n], ALU.mult)
                    else:
                        self.tt('dve', prt[:, 0:n], pb[:, 0:n], sgt[:, 0:n], ALU.mult)
                        if nb == 1:
                            self.tt('pool', acc[:, ti, 0:n], acc[:, ti, 0:n], prt[:, 0:n], ALU.add)
                        else:
                            self.tt('pool', mg[:, m, c:c + n], acc[:, ti, 0:n], prt[:, 0:n], ALU.add)
        self.tap('mg_%d_%d' % (l, pi), mg, eng='pool')
        for q in range(4):
            (wo,) = self.wget([self.wspec_cols(d['w_out'][l], 256 * q, 256)])
            for jj in range(2):
                m = 2 * q + jj
                for (c, n, kind) in tiles:
                    py = self.pbank()
                    self.mmg(py[:, 0:n], [(wo[:, kc, jj * 128:(jj + 1) * 128], mg[:, kc, c:c + n]) for kc in range(KC)])
                    self.tt('dve', self.xT[:, m, xc0 + c:xc0 + c + n], py[:, 0:n], self.xT[:, m, xc0 + c:xc0 + c + n], ALU.add)

    def pool_branch(self, l, pi, u, o_pool, tiles, Lp, has_s, W):
        d = self.d
        w_in = d['w_in'][l]
        E = 15 + Lp
        xp = self.alloc([E], F32)
        A = self.alloc([E], F32)
        Bb = self.alloc([E], F32)
        dT = self.alloc([W], BF16)
        t16 = self.alloc([16], F32)
        if has_s:
            ES = 19 * 16
            xps = self.alloc([ES], F32)
            As = self.alloc([ES], F32)
            Bs = self.alloc([ES], F32)
            stg = [self.alloc([512], F32) for _ in range(2)]
            stS = self.alloc([512], F32)
            for t in range(15):
                self.dma('sp', stg[t // 8][16 * (t % 8):16 * (t % 8) + 16, :], d['st_pool'][l][:, t, :])
        (pwr,) = self.wget([(d['pool_w'][l].rearrange("g c e -> c g e"), 4, 128)])
        pw = self.alloc([4, 128], BF16)
        self.tcopy('pool', pw[:, :, :], pwr[:, :, :])
        for gi in range(4):
            wsz = (2, 4, 8, 16)[gi]
            (wp,) = self.wget([self.wspec_cols(w_in, gi * 128, 128)])
            if pi == 0:
                self.memset('pool', xp[:, 0:15], 0.0)
            else:
                self.tcopy('pool', xp[:, 0:15], self.tails_pool[:, gi, :])
            for (c, n, kind) in tiles:
                ps = self.pbank()
                self.mmg(ps[:, 0:n], [(wp[:, kc, :], u[:, kc, c:c + n]) for kc in range(KC)])
                if kind == 'p':
                    self.tcopy('act', xp[:, 15 + c:15 + c + n], ps[:, 0:n])
                else:
                    self.tcopy('act', xps[:, 240:304], ps[:, 0:n])
            self.tcopy('pool', self.tails_pool[:, gi, :], xp[:, Lp:Lp + 15])
            if has_s:
                ps = self.pbank()
                self.tr(ps[:, 0:128], stg[0][0:128, gi * 128:(gi + 1) * 128], self.ident(128))
                self.tr(ps[:, 128:240], stg[1][0:112, gi * 128:(gi + 1) * 128], self.ident(112))
                self.tcopy('dve', xps[:, 0:240], ps[:, 0:240])
                ps = self.pbank()
                self.tr(ps[0:64, 0:128], xps[:, 240:304], self.ident(128))
                self.tcopy('act', stS[0:64, gi * 128:(gi + 1) * 128], ps[0:64, 0:128])
            src, srcs = xp, (xps if has_s else None)
            bufs, bufss = [A, Bb], ([As, Bs] if has_s else None)
            for k in range(1, gi + 2):
                dst = bufs[(k - 1) % 2]
                i0 = 2 ** k - 1
                sh = 2 ** (k - 1)
                self.tt('dve', dst[:, i0:E], src[:, i0:E], src[:, i0 - sh:E - sh], ALU.add)
                src = dst
                if has_s:
                    dsts = bufss[(k - 1) % 2]
                    self.tt('pool', dsts[:, 16 * i0:ES], srcs[:, 16 * i0:ES], srcs[:, 16 * (i0 - sh):ES - 16 * sh], ALU.add)
                    srcs = dsts
            invw = 1.0 / wsz
            self.stt(dT[:, 0:Lp], src[:, 15:15 + Lp], invw, xp[:, 15:15 + Lp], ALU.mult, ALU.subtract)
            if pi == 0:
                self.tt('dve', t16[:, :], src[:, 15:31], self.cst[:, C_INVCNT + gi * 16:C_INVCNT + gi * 16 + 16], ALU.mult)
                self.tt('dve', dT[:, 0:16], t16[:, :], xp[:, 15:31], ALU.subtract)
            if has_s:
                self.stt(dT[:, Lp:Lp + 64], srcs[:, 240:304], invw, xps[:, 240:304], ALU.mult, ALU.subtract)
            if gi == 3 and pi == 0:
                self.tap('dbgxp', xp); self.tap('dbgA', A); self.tap('dbgB', Bb); self.tap('dbgdT', dT, eng='pool')
            for (c, n, kind) in tiles:
                ps = self.pbank()
                self.mm(ps[:, 0:n], pw[:, gi, :], dT[:, c:c + n])
                self.ts('dve', o_pool[:, gi, c:c + n], ps[:, 0:n], self.vec[:, l, V_PSCALE + gi:V_PSCALE + gi + 1], ALU.mult)
        if pi == 1:
            ps = self.pbank()
            for gi in range(4):
                self.tr(ps[0:15, gi * 128:(gi + 1) * 128], self.tails_pool[:, gi, :], self.ident(128))
            sto = self.alloc([512], F32)
            self.tcopy('dve', sto[0:15, :], ps[0:15, :])
            self.dma('sp', d['npool_p'][l], sto[0:15, :], is_out=True)
            for t in range(4):
                self.dma('sp', d['npool_s'][l][:, 11 + t, :], stS[16 * t:16 * t + 16, :], is_out=True)

    def cm_branch(self, l, pi, u, o_cm, tiles, Lp, has_s, W):
        d = self.d
        w_in = d['w_in'][l]
        E = 30 + Lp
        hb = self.alloc([4, E], BF16)
        dg = self.alloc([31, 128], BF16)
        sig = self.alloc([512], F32)
        if has_s:
            ES = 34 * 16
            hbs = self.alloc([4, ES], BF16)
            stS = self.alloc([512], F32)
            hs32 = self.alloc([4, 64], F32)
        cm_mark = self.top
        if has_s:
            stg = [self.alloc([512], F32) for _ in range(4)]
            for t in range(30):
                self.dma('sp' if t % 2 else 'act', stg[t // 8][16 * (t % 8):16 * (t % 8) + 16, :], d['st_cm'][l][:, t, :])
        for c4 in range(4):
            wa, wg = self.wget([self.wspec_cols(w_in, 2568 + c4 * 128, 128), self.wspec_cols(w_in, 2568 + 512 + c4 * 128, 128)])
            if pi == 0:
                self.memset('pool', hb[:, c4, 0:30], 0.0)
            else:
                self.tcopy('pool', hb[:, c4, 0:30], self.tails_cm[:, c4, :])
            for (c, n, kind) in tiles:
                pa = self.pbank()
                pg = self.pbank()
                self.mmg(pa[:, 0:n], [(wa[:, kc, :], u[:, kc, c:c + n]) for kc in range(KC)])
                self.mmg(pg[:, 0:n], [(wg[:, kc, :], u[:, kc, c:c + n]) for kc in range(KC)])
                self.act(sig[:, 0:n], pg[:, 0:n], AF.Sigmoid)
                if kind == 'p':
                    self.tt('dve', hb[:, c4, 30 + c:30 + c + n], pa[:, 0:n], sig[:, 0:n], ALU.mult)
                else:
                    self.tt('dve', hs32[:, c4, 0:64], pa[:, 0:n], sig[:, 0:n], ALU.mult)
                    self.tcopy('pool', hbs[:, c4, 480:544], hs32[:, c4, 0:64])
            self.tcopy('pool', self.tails_cm[:, c4, :], hb[:, c4, Lp:Lp + 30])
        for c4 in range(4):
            if has_s:
                for blk in range(4):
                    nr = 128 if blk < 3 else 96
                    ps = self.pbank()
                    self.tr(ps[:, 0:nr], stg[blk][0:nr, c4 * 128:(c4 + 1) * 128], self.ident(nr))
                    self.tcopy('dve', hbs[:, c4, blk * 128:blk * 128 + nr], ps[:, 0:nr])
                ps = self.pbank()
                self.tr(ps[0:64, 0:128], hs32[:, c4, 0:64], self.ident(128))
                self.tcopy('act', stS[0:64, c4 * 128:(c4 + 1) * 128], ps[0:64, 0:128])
        self.top = cm_mark
        yall = self.alloc([4, W], F32)
        for c4 in range(4):
            for k in range(31):
                self.ts('dve', dg[:, k, :], self.idb[:, :],
                        self.vec[:, l, V_CMW + c4 * 31 + k:V_CMW + c4 * 31 + k + 1], ALU.mult)
            for (c, n, kind) in tiles:
                ps = self.pbank()
                if kind == 'p':
                    self.mmg(ps[:, 0:n], [(dg[:, k, :], hb[:, c4, c + k:c + k + n]) for k in range(31)])
                else:
                    self.mmg(ps[:, 0:n], [(dg[:, k, :], hbs[:, c4, 16 * k:16 * k + 64]) for k in range(31)])
                self.act(yall[:, c4, c:c + n], ps[:, 0:n], AF.Identity, bias=self.vec[:, l, V_CMB + c4:V_CMB + c4 + 1])
        ones32 = self.cst[:, C_ONES:C_ONES + 128]
        yn = self.alloc([512], F32)
        for (c, n, kind) in tiles:
            ps = self.pbank()
            self.mmg(ps[:, 0:n], [(ones32, yall[:, c4, c:c + n]) for c4 in range(4)])
            mu = sig
            self.act(mu[:, 0:n], ps[:, 0:n], AF.Copy, scale=-1.0 / 512)
            for c4 in range(4):
                self.tt('pool', yall[:, c4, c:c + n], yall[:, c4, c:c + n], mu[:, 0:n], ALU.add)
            rs = self.rstd_tile([yall[:, c4, c:c + n] for c4 in range(4)], n, 1.0 / 512, LN_EPS)
            for c4 in range(4):
                self.stt(yn[:, 0:n], yall[:, c4, c:c + n], self.vec[:, l, V_LNG + c4:V_LNG + c4 + 1], rs, ALU.mult, ALU.mult)
                self.act(o_cm[:, c4, c:c + n], yn[:, 0:n], AF.Silu, bias=self.vec[:, l, V_LNB + c4:V_LNB + c4 + 1])
        if pi == 1:
            t32 = self.alloc([4, 32], F32)
            self.tcopy('dve', t32[:, :, 0:30], self.tails_cm[:, :, :])
            ps = self.pbank()
            for c4 in range(4):
                self.tr(ps[0:30, c4 * 128:(c4 + 1) * 128], t32[:, c4, 0:30], self.ident(128))
            sto = self.alloc([512], F32)
            self.tcopy('dve', sto[0:30, :], ps[0:30, :])
            self.dma('sp', d['ncm_p'][l], sto[0:30, :], is_out=True)
            for t in range(4):
                self.dma('sp', d['ncm_s'][l][:, 26 + t, :], stS[16 * t:16 * t + 16, :], is_out=True)

    def dn_branch(self, l, pi, u, o_dn, tiles, Lp, has_s, W):
        d = self.d
        w_in = d['w_in'][l]
        cst = self.cst
        chunks = [(i * 128, 128, 'p') for i in range(Lp // 128)] + ([(Lp, NSM, 's')] if has_s else [])
        nch = len(chunks)
        sg_all = self.alloc([W], F32)
        gcT = self.alloc([W], F32)
        toks = self.alloc([nch, 8], F32)
        gct = self.alloc([nch, 4], F32)
        glt = self.alloc([nch, 4], F32)
        Gt = self.alloc([nch, 4], F32)
        ktt = self.alloc([nch, 4], F32)
        nbg = self.alloc([nch, 4], F32)
        tmp4 = self.alloc([4], F32)
        negA = self.alloc([8], F32)
        dselA = self.alloc([8], F32)
        mark = self.top
        sp_all = self.alloc([W], F32)
        e1 = self.alloc([512], F32)
        self.act(negA[0:8, 0:1], self.vec[0:8, l, V_ALOG:V_ALOG + 1], AF.Exp)
        self.ts('dve', dselA[0:8, 0:8], cst[0:8, C_DSELA:C_DSELA + 8], negA[0:8, 0:1], ALU.mult, -1.0, ALU.mult)
        (wab,) = self.wget([self.wspec_cols(w_in, 2560, 8)])
        for (c, n, kind) in tiles:
            ps = self.pbank()
            self.mmg(ps[0:8, 0:n], [(wab[:, kc, 0:8], u[:, kc, c:c + n]) for kc in range(KC)])
            self.act(e1[0:8, 0:n], ps[0:8, 0:n], AF.Exp, bias=self.vec[0:8, l, V_DTB:V_DTB + 1])
            self.act(sp_all[0:8, c:c + n], e1[0:8, 0:n], AF.Ln, bias=self.epsv(1.0, 8))
            self.act(sg_all[0:8, c:c + n], ps[0:8, 0:n], AF.Sigmoid)
        for ci, (c, n, kind) in enumerate(chunks):
            ps = self.pbank()
            self.mm(ps[0:n, 0:8], sp_all[0:8, c:c + n], dselA[0:8, 0:8], start=True, stop=False, count=False)
            self.mm(ps[0:n, 0:8], sg_all[0:8, c:c + n], cst[0:8, C_DSELB:C_DSELB + 8], start=False, stop=True)
            self.tcopy('dve', toks[0:n, ci, :], ps[0:n, 0:8])
            m1 = cst[0:n, C_M1:C_M1 + n] if kind == 'p' else cst[0:n, C_M1S:C_M1S + n]
            al = cst[0:n, C_ONES:C_ONES + n] if kind == 'p' else cst[0:n, C_ALLS:C_ALLS + n]
            ps2 = self.pbank()
            self.mm(ps2[0:n, 0:4], m1, toks[0:n, ci, 0:4])
            self.mm(ps2[0:n, 4:8], al, toks[0:n, ci, 0:4])
            self.tcopy('dve', gct[0:n, ci, :], ps2[0:n, 0:4])
            self.tcopy('dve', glt[0:n, ci, :], ps2[0:n, 4:8])
            ps3 = self.pbank()
            self.mm(ps3[0:8, 0:n], toks[0:n, ci, 0:8], m1)
            self.tcopy('act', gcT[0:8, c:c + n], ps3[0:8, 0:n])
            self.act(Gt[0:n, ci, :], gct[0:n, ci, :], AF.Exp)
            self.tt('dve', tmp4[0:n, :], glt[0:n, ci, :], gct[0:n, ci, :], ALU.subtract)
            self.act(ktt[0:n, ci, :], tmp4[0:n, :], AF.Exp)
            self.stt(nbg[0:n, ci, :], toks[0:n, ci, 4:8], -1.0, Gt[0:n, ci, :], ALU.mult, ALU.mult)
        self.top = mark
        qT = self.alloc([W], BF16)
        kT = self.alloc([W], BF16)
        vT = self.alloc([W], BF16)
        kbT = self.alloc([W], BF16)
        qdT = self.alloc([W], BF16)
        gcb = self.alloc([W], F32)
        o_raw = vT
        Ktl = self.alloc([nch, 128], BF16)
        VB = self.alloc([nch, 128], BF16)
        AttnT = self.alloc([nch, 128], BF16)
        Yf = self.alloc([nch, 128], BF16)
        Rr = self.alloc([128], BF16)
        vn = self.alloc([128], BF16)
        Sbf = self.alloc([128], BF16)
        if has_s:
            stq = self.alloc([3, 128], F32)
            stqo = self.alloc([3, 128], F32)
            sq48 = self.alloc([3, 48], F32)
            Ssb = self.alloc([16, 128], BF16)
            Ssf2 = [self.alloc([4, 128], F32) for _ in range(2)]
            Km = self.alloc([2, 128], BF16)
            Rrs = self.alloc([128], BF16)
            vns = self.alloc([128], BF16)
            self.memset('pool', Km[:, :, :], 0.0)
            self.memset('pool', vns[:, :], 0.0)
            KSTs = self.alloc([64], F32)
            otmp = self.alloc([64], F32)
        umark = self.top
        dg = self.alloc([12, 128], BF16)
        slf = self.alloc([512], F32)
        Gbt = self.alloc([512], F32)
        slf2 = Gbt
        pre3 = self.alloc([3, 3 + Lp], BF16)
        sqb = self.alloc([2, W], BF16)
        if has_s:
            pres3 = self.alloc([3, 112], BF16)
        atop = self.top
        self.top = umark
        GMAX = 4
        ABZ = [[self.alloc([3, 128], F32) for _ in range(2)] for _ in range(GMAX)]
        decTg = [self.alloc([128], F32) for _ in range(GMAX)]
        self.top = max(self.top, atop)
        ev_cnt = [0]

        def evac(out, in_):
            ev_cnt[0] += 1
            self.tcopy('act' if ev_cnt[0] % 2 else 'dve', out, in_)

        for h in range(4):
            ws = self.wget([self.wspec_cols(w_in, 512 + idx * 512 + h * 128, 128) for idx in range(3)])
            if has_s:
                for t in range(3):
                    src = d['st_dnc'][l][:, t, :].rearrange("s (i hh dd) -> s i hh dd", i=3, hh=4)[:, :, h, :]
                    self.dma('sp', stq[16 * t:16 * t + 16, :, :], src)
                self.dma('pool', Ssb[:, :, :], d['st_dn'][l][:, h].rearrange("s k v -> k s v"))
            for idx in range(3):
                for k in range(4):
                    col = V_DNW + (4 * idx + h) * 4 + k
                    self.ts('dve', dg[:, idx * 4 + k, :], self.idb[:, :], self.vec[:, l, col:col + 1], ALU.mult)
            for idx in range(3):
                ch = 4 * idx + h
                if pi == 0:
                    self.memset('pool', pre3[:, idx, 0:3], 0.0)
                else:
                    self.tcopy('pool', pre3[:, idx, 0:3], self.tails_dn[:, ch, :])
                for (c, n, kind) in tiles:
                    ps = self.pbank()
                    self.mmg(ps[:, 0:n], [(ws[idx][:, kc, :], u[:, kc, c:c + n]) for kc in range(KC)])
                    if kind == 'p':
                        self.tcopy('act', pre3[:, idx, 3 + c:3 + c + n], ps[:, 0:n])
                        if c + n == Lp:
                            self.tcopy('dve', self.tails_dn[:, ch, :], ps[:, n - 3:n])
                    else:
                        self.tcopy('act', pres3[:, idx, 48:112], ps[:, 0:n])
                        self.tcopy('dve', sq48[:, idx, :], ps[:, 16:64])
            if has_s:
                ps = self.pbank()
                for idx in range(3):
                    self.tr(ps[:, idx * 48:idx * 48 + 48], stq[0:48, idx, :], self.ident(48))
                for idx in range(3):
                    self.tcopy('dve', pres3[:, idx, 0:48], ps[:, idx * 48:idx * 48 + 48])
            dstT = [qT, kT, vT]
            for idx in range(3):
                for (c, n, kind) in tiles:
                    ps = self.pbank()
                    if kind == 'p':
                        self.mmg(ps[:, 0:n], [(dg[:, idx * 4 + k, :], pre3[:, idx, c + k:c + k + n]) for k in range(4)])
                    else:
                        self.mmg(ps[:, 0:n], [(dg[:, idx * 4 + k, :], pres3[:, idx, 16 * k:16 * k + 64]) for k in range(4)])
                    self.act(dstT[idx][:, c:c + n], ps[:, 0:n], AF.Silu)
            for idx in range(2):
                for (c, n, kind) in tiles:
                    self.act(sqb[:, idx, c:c + n], dstT[idx][:, c:c + n], AF.Square)
            pss = []
            for idx in range(2):
                for (c, n, kind) in tiles:
                    pss.append((idx, c, n, self.rstd_psum([sqb[:, idx, c:c + n]], n, 1.0, 1e-6)))
            for (idx, c, n, ps) in pss:
                self.rstd_psum_fin(ps, n, 1.0, 1e-6, stage=0)
            for (idx, c, n, ps) in pss:
                self.rstd_psum_fin(ps, n, 1.0, 1e-6, bias2=(-0.5 * float(np.log(128.0)) if idx == 0 else 0.0), stage=1)
            for (idx, c, n, ps) in pss:
                self.tt('dve', dstT[idx][:, c:c + n], dstT[idx][:, c:c + n], ps[:, 0:n], ALU.mult)
            if has_s:
                ps = self.pbank()
                for idx in range(3):
                    self.tr(ps[0:48, idx * 128:(idx + 1) * 128], sq48[:, idx, :], self.ident(128))
                for idx in range(3):
                    self.tcopy('dve', stqo[0:48, idx, :], ps[0:48, idx * 128:(idx + 1) * 128])
                for t in range(3):
                    dst = d['ndnc_s'][l][:, t, :].rearrange("s (i hh dd) -> s i hh dd", i=3, hh=4)[:, :, h, :]
                    self.dma('sp', dst, stqo[16 * t:16 * t + 16, :, :], is_out=True)
            ci = 0
            for (c, n, kind) in tiles:
                ps = self.pbank()
                self.mm(ps[:, 0:n], cst[0:8, C_SEL + h * 128:C_SEL + (h + 1) * 128], gcT[0:8, c:c + n])
                self.tcopy('dve', gcb[:, c:c + n], ps[:, 0:n])
                self.act(Gbt[:, 0:n], ps[:, 0:n], AF.Exp)
                ps2 = self.pbank()
                self.mm(ps2[:, 0:n], cst[0:8, C_SEL + (4 + h) * 128:C_SEL + (5 + h) * 128], sg_all[0:8, c:c + n])
                self.tt('dve', kbT[:, c:c + n], kT[:, c:c + n], ps2[:, 0:n], ALU.mult)
                self.tt('pool', qdT[:, c:c + n], qT[:, c:c + n], Gbt[:, 0:n], ALU.mult)
                if kind == 'p':
                    for j in range(n // 128):
                        self.tcopy('pool', self.glc[:, ci:ci + 1], Gbt[:, j * 128 + 127:j * 128 + 128])
                        ci += 1
                else:
                    self.tcopy('pool', self.gls[:, 0:16], Gbt[:, 48:64])
            if pi == 0:
                self.memset('pool', self.Sst[:, h, :], 0.0)
            self.tcopy('act', Sbf[:, :], self.Sst[:, h, :])

            def pre_steps(ci, gslot):
                c, n, kind = chunks[ci]
                negm = cst[0:n, C_NEGI:C_NEGI + n] if kind == 'p' else cst[0:n, C_NEGIS:C_NEGIS + n]
                strm = cst[0:n, C_STRICT:C_STRICT + n] if kind == 'p' else cst[0:n, C_STRICTS:C_STRICTS + n]
                nlev = 6 if n == 128 else 1
                dT_ = decTg[gslot]
                tl = ABZ[gslot]
                steps = []

                def s0a():
                    psb = self.pbank_bf()
                    self.tr(psb[0:n, 0:128], kT[:, c:c + n], self.idb[:, :])
                    self.tr(psb[0:n, 128:256], vT[:, c:c + n], self.idb[:, :])
                    self.ts('dve', Ktl[0:n, ci, :], psb[0:n, 0:128], ktt[0:n, ci, h:h + 1], ALU.mult)
                    self.ts('dve', VB[0:n, ci, :], psb[0:n, 128:256], toks[0:n, ci, 4 + h:5 + h], ALU.mult)
                    self.stt(dT_[0:n, 0:n], gcb[0:n, c:c + n], gct[0:n, ci, h:h + 1], negm, ALU.subtract, ALU.add)
                    self.act(dT_[0:n, 0:n], dT_[0:n, 0:n], AF.Exp)
                steps.append(s0a)

                def s0b():
                    psq = self.pbank()
                    self.mm(psq[0:n, 0:n], kT[:, c:c + n], qT[:, c:c + n])
                    self.mm(psq[0:n, 128:128 + n], kT[:, c:c + n], kbT[:, c:c + n])
                    self.tt('dve', AttnT[0:n, ci, 0:n], psq[0:n, 0:n], dT_[0:n, 0:n], ALU.mult)
                    self.stt(tl[0][0:n, 1, 0:n], psq[0:n, 128:128 + n], -1.0, dT_[0:n, 0:n], ALU.mult, ALU.mult)
                    self.tt('pool', tl[0][0:n, 1, 0:n], tl[0][0:n, 1, 0:n], strm, ALU.mult)
                    self.tcopy('pool', tl[0][0:n, 2, 0:n], cst[0:n, C_ID:C_ID + n])
                steps.append(s0b)

                def s2():
                    psa = self.pbank()
                    self.tr(psa[0:n, 0:n], tl[0][0:n, 1, 0:n], self.ident(n))
                    evac(tl[0][0:n, 0, 0:n], psa[0:n, 0:n])
                steps.append(s2)

                def mk(lev):
                    def st():
                        src = tl[(lev - 1) % 2]
                        last = (lev == nlev + 1)
                        dstt = tl[lev % 2]
                        ps = self.pbank()
                        A_, B_, Z_ = src[0:n, 0, 0:n], src[0:n, 1, 0:n], src[0:n, 2, 0:n]
                        if not last:
                            self.mm(ps[0:n, 0:n], B_, A_)
                            if lev < nlev:
                                self.mm(ps[0:n, 128:128 + n], A_, B_)
                        self.mm(ps[0:n, 256:256 + n], A_, Z_)
                        if last:
                            self.tt('dve', Yf[0:n, ci, 0:n], ps[0:n, 256:256 + n], Z_, ALU.add)
                        else:
                            self.tcopy('act', dstt[0:n, 0, 0:n], ps[0:n, 0:n])
                            if lev < nlev:
                                self.tcopy('act', dstt[0:n, 1, 0:n], ps[0:n, 128:128 + n])
                            self.tt('dve', dstt[0:n, 2, 0:n], ps[0:n, 256:256 + n], Z_, ALU.add)
                    return st
                for lev in range(1, nlev + 2):
                    steps.append(mk(lev))
                return steps

            def t6_stages(ci):
                c, n, kind = chunks[ci]
                stages = []
                if kind == 'p':
                    def a1():
                        ps = self.pbank()
                        self.mm(ps[:, 0:128], kT[:, c:c + n], Sbf[:, :])
                        self.stt(Rr[:, :], ps[:, 0:128], nbg[:, ci, h:h + 1], VB[:, ci, :], ALU.mult, ALU.add)

                    def a2():
                        ps2 = self.pbank()
                        self.mm(ps2[:, 0:128], Yf[:, ci, :], Rr[:, :])
                        self.tcopy('act', vn[:, :], ps2[:, 0:128])

                    def a3():
                        ps3 = self.pbank()
                        self.mm(ps3[:, 0:128], Sbf[:, :], qdT[:, c:c + n], start=True, stop=False, count=False)
                        self.mm(ps3[:, 0:128], vn[:, :], AttnT[:, ci, :], start=False, stop=True)
                        ps4 = self.pbank()
                        self.mm(ps4[:, 0:128], Ktl[:, ci, :], vn[:, :])
                        self.stt(self.Sst[:, h, :], self.Sst[:, h, :], self.glc[:, ci:ci + 1], ps4[:, 0:128], ALU.mult, ALU.add)
                        self.tcopy('act', Sbf[:, :], self.Sst[:, h, :])
                        self.tcopy('act', o_raw[:, c:c + n], ps3[:, 0:128])
                    stages += [a1, a2, a3]
                else:
                    def b1():
                        psK = self.pbank()
                        for s in range(16):
                            self.mm(psK[:, s:64:16], Ssb[:, s, :], kT[:, Lp + s:Lp + 64:16], count=(s == 15))
                        self.tcopy('dve', KSTs[:, 0:64], psK[:, 0:64])

                    def b2():
                        psT = self.pbank()
                        self.tr(psT[0:64, 0:128], KSTs[:, 0:64], self.ident(128))
                        self.stt(Rrs[0:64, :], psT[0:64, 0:128], nbg[0:64, ci, h:h + 1], VB[0:64, ci, :], ALU.mult, ALU.add)

                    def b3():
                        ps2 = self.pbank()
                        self.mm(ps2[0:64, 0:128], Yf[0:64, ci, 0:64], Rrs[0:64, :])
                        self.tcopy('act', vns[0:64, :], ps2[0:64, 0:128])

                    def b4():
                        psQ = self.pbank()
                        for s in range(16):
                            self.mm(psQ[:, s:64:16], Ssb[:, s, :], qdT[:, Lp + s:Lp + 64:16], count=(s == 15))
                        self.tcopy('act', otmp[:, 0:64], psQ[:, 0:64])
                        psA = self.pbank()
                        self.mm(psA[:, 0:64], vns[0:64, :], AttnT[0:64, ci, 0:64])
                        self.tt('dve', o_raw[:, c:c + 64], otmp[:, 0:64], psA[:, 0:64], ALU.add)
                    stages += [b1, b2, b3, b4]

                    def mkq(qt):
                        def bq():
                            Ssf = Ssf2[qt % 2]
                            if qt == 0:
                                self.dma('sp', Ssf2[0][:, :, :], d['st_dn'][l][0:4, h].rearrange("s k v -> k s v"))
                            if qt < 3:
                                self.dma('sp', Ssf2[(qt + 1) % 2][:, :, :], d['st_dn'][l][4 * qt + 4:4 * qt + 8, h].rearrange("s k v -> k s v"))
                            for s4 in range(4):
                                s = 4 * qt + s4
                                self.ts('dve', Km[0:64, s % 2, :], Ktl[0:64, ci, :], cst[0:64, C_SEQM + s:C_SEQM + s + 1], ALU.mult)
                                ps = self.pbank()
                                self.mm(ps[:, 0:128], Km[:, s % 2, :], vns[:, :])
                                self.stt(Ssf[:, s4, :], Ssf[:, s4, :], self.gls[:, s:s + 1], ps[:, 0:128], ALU.mult, ALU.add)
                            self.dma('sp', d['ndn_s'][l][4 * qt:4 * qt + 4, h].rearrange("s k v -> k s v"), Ssf[:, :, :], is_out=True)
                        return bq
                    stages += [mkq(qt) for qt in range(4)]
                return stages

            if nch > 8:
                groups = [[8, 0, 1, 2], [3, 4, 5, 6], [7]]
            else:
                groups = [[0, 1, 2, 3], [4, 5, 6, 7]]
            bg = []
            for grp in groups:
                lists = [pre_steps(ci, gi) for gi, ci in enumerate(grp)]
                nst = max(len(x) for x in lists)
                for si in range(nst):
                    for x in lists:
                        if si < len(x):
                            x[si]()
                    for _ in range(2):
                        if bg:
                            bg.pop(0)()
                while bg:
                    bg.pop(0)()
                ssts = []
                psts = []
                for ci in grp:
                    (ssts if chunks[ci][2] == 's' else psts).extend(t6_stages(ci))
                while ssts or psts:
                    if psts:
                        bg.append(psts.pop(0))
                    if ssts:
                        bg.append(ssts.pop(0))
            while bg:
                bg.pop(0)()
            if pi == 1:
                self.dma('sp', d['ndn_p'][l][h], self.Sst[:, h, :], is_out=True)
            (wz,) = self.wget([self.wspec_cols(w_in, 2048 + h * 128, 128)])
            for (c, n, kind) in tiles:
                ps = self.pbank()
                self.mmg(ps[:, 0:n], [(wz[:, kc, :], u[:, kc, c:c + n]) for kc in range(KC)])
                self.act(slf[:, 0:n], ps[:, 0:n], AF.Silu)
                self.act(self.tmp_sq[:, 0, 0:n], o_raw[:, c:c + n], AF.Square)
                prs = self.rstd_psum([self.tmp_sq[:, 0, 0:n]], n, 1.0 / 128, RMS_EPS)
                self.rstd_psum_fin(prs, n, 1.0 / 128, RMS_EPS, stage=0)
                self.rstd_psum_fin(prs, n, 1.0 / 128, RMS_EPS, stage=1)
                self.tt('dve', slf2[:, 0:n], o_raw[:, c:c + n], prs[:, 0:n], ALU.mult)
                self.stt(o_dn[:, h, c:c + n], slf2[:, 0:n], self.vec[:, l, V_NORMG:V_NORMG + 1], slf[:, 0:n], ALU.mult, ALU.mult)
        if pi == 1:
            sto = self.alloc([1536], F32)
            for g in range(3):
                ps = self.pbank()
                for j in range(4):
                    self.tr(ps[0:3, j * 128:(j + 1) * 128], self.tails_dn[:, 4 * g + j, :], self.ident(128))
                self.tcopy('dve', sto[0:3, g * 512:(g + 1) * 512], ps[0:3, :])
            self.dma('sp', d['ndnc_p'][l], sto[0:3, :], is_out=True)


def build_program(debug=None, stop_after=None, names=None):
    b = Builder(debug=debug, stop_after=stop_after)
    b.p.names = names
    nc = b.build()
    return nc, b


def make_in_maps(inp, cores):
    consts = make_consts()
    vecs = make_vecs(inp)
    f = lambda a: np.ascontiguousarray(a, dtype=np.float32)
    maps = []
    for c in cores:
        sl = slice(16 * c, 16 * c + 16)
        maps.append({
            'xp': f(inp['x_prompt'][c]), 'xs': f(inp['x_sample'][sl]),
            'st_pool': f(inp['state_pool'][:, sl]), 'st_dnc': f(inp['state_dn_conv'][:, sl]),
            'st_dn': f(inp['state_dn'][:, sl]), 'st_cm': f(inp['state_cm_conv'][:, sl]),
            'pp': f(inp['p_prompt'][:, c]), 'psm': f(inp['p_sample'][:, sl]),
            'w_up1': f(inp['w_ffn1_up']), 'w_dn1': f(inp['w_ffn1_down']),
            'w_in': f(inp['w_in']), 'pool_w': f(inp['pool_w']),
            'w_br': f(inp['w_branch']), 'w_out': f(inp['w_out']),
            'w_up2': f(inp['w_ffn2_up']), 'w_dn2': f(inp['w_ffn2_down']),
            'w_pg': f(inp['w_ple_gate']), 'w_pp': f(inp['w_ple_proj']),
            'vecs': vecs, 'consts': consts,
        })
    return maps


def kernel(**inp):
    inp = {k: np.asarray(v) for k, v in inp.items()}
    nc, _ = build_program()
    maps = make_in_maps(inp, list(range(NCORE)))
    res = run_bass_kernel_spmd(nc, maps, core_ids=list(range(NCORE)))
    R = res.results
    y_p = np.stack([R[c]['y_p'] for c in range(NCORE)], 0)
    y_s = np.concatenate([R[c]['y_s'] for c in range(NCORE)], 0)

    def pcat(name):
        return np.stack([R[c][name] for c in range(NCORE)], 1)

    def scat(name):
        return np.concatenate([R[c][name] for c in range(NCORE)], 1)

    outs = (y_p, y_s, pcat('npool_p'), pcat('ndnc_p'), pcat('ndn_p'), pcat('ncm_p'),
            scat('npool_s'), scat('ndnc_s'), scat('ndn_s'), scat('ncm_s'))
    return tuple(np.ascontiguousarray(o, dtype=np.float32) for o in outs)
```
